# Optimizing a Trainium2 kernel written in Bass

```python
import math
import jax, jax.numpy as jnp
from jax import lax
import numpy as np

D_MODEL = 1024
BATCH = 8
SEQ = 2048
DEPTH = 2

HALF = D_MODEL // 2
HGRN_HEADS = 4
HGRN_DIM = HALF // HGRN_HEADS
HGRN_CHUNK = 64
FOX_HEADS = 8
FOX_DIM = HALF // FOX_HEADS
LRU_WIDTH = HALF
LRU_BLOCKS = 8
LRU_BLOCK_DIM = LRU_WIDTH // LRU_BLOCKS
LRU_C = 8.0
CONV_WIDTH = 4
SB_HEADS = 8
SB_DIM = HALF // SB_HEADS
Q_BLOCK = 128
D_FF = -(-8 * D_MODEL // (3 * 256)) * 256
N_EVEN = (DEPTH + 1) // 2
N_ODD = DEPTH // 2
ALPHA = (2 * DEPTH) ** 0.25
BETA = (8 * DEPTH) ** -0.25
EVEN_IN = 7 * HALF + FOX_HEADS
ODD_IN = 2 * LRU_WIDTH + 3 * HALF
EPS = 1e-5

kernel_name = 'hybrid_hgrn2_fox_rglru_stickbreaking'


def layer_norm(x, g, b):
    xf = x.astype(jnp.float32)
    mu = jnp.mean(xf, axis=-1, keepdims=True)
    xc = xf - mu
    var = jnp.mean(xc * xc, axis=-1, keepdims=True)
    y = xc * lax.rsqrt(var + EPS) * g.astype(jnp.float32) + b.astype(jnp.float32)
    return y.astype(x.dtype)


def rms_norm_f32(x, g):
    xf = x.astype(jnp.float32)
    return xf * lax.rsqrt(jnp.mean(xf * xf, axis=-1, keepdims=True) + EPS) * g.astype(jnp.float32)


def split_cols(t, sizes):
    return jnp.split(t, np.cumsum(sizes)[:-1].tolist(), axis=-1)


def hgrn2(q, f_logit, inp, g, lb, norm_g):
    B, S, _ = q.shape
    nc = S // HGRN_CHUNK
    z = f_logit.astype(jnp.float32)
    log_f = jnp.log(lb + (1.0 - lb) * jax.nn.sigmoid(z))
    k = (1.0 - lb) * jax.nn.sigmoid(-z)

    def to_chunks(t):
        return t.astype(jnp.float32).reshape(B, nc, HGRN_CHUNK, HGRN_HEADS, HGRN_DIM).transpose(1, 0, 3, 2, 4)

    qc, kc, vc, lc = to_chunks(q), to_chunks(k), to_chunks(inp), to_chunks(log_f)
    causal = jnp.tril(jnp.ones((HGRN_CHUNK, HGRN_CHUNK), dtype=bool))[:, :, None]

    def step(state, chunk):
        qt, kt, vt, lt = chunk
        b = jnp.cumsum(lt, axis=2)
        o_inter = jnp.einsum('bhtk,bhkv->bhtv', qt * jnp.exp(b), state)
        rel = b[:, :, :, None, :] - b[:, :, None, :, :]
        decay = jnp.exp(jnp.where(causal, rel, -jnp.inf))
        scores = jnp.einsum('bhtk,bhtsk,bhsk->bhts', qt, decay, kt)
        o = o_inter + jnp.einsum('bhts,bhsv->bhtv', scores, vt)
        b_last = b[:, :, -1:, :]
        state = jnp.exp(b_last[:, :, 0, :])[..., None] * state + jnp.einsum(
            'bhsk,bhsv->bhkv', kt * jnp.exp(b_last - b), vt)
        return state, o

    s0 = jnp.zeros((B, HGRN_HEADS, HGRN_DIM, HGRN_DIM), jnp.float32)
    _, o = lax.scan(step, s0, (qc, kc, vc, lc))
    o = o.transpose(1, 0, 3, 2, 4).reshape(B, S, HGRN_HEADS, HGRN_DIM)
    o = rms_norm_f32(o, norm_g)
    gate = jax.nn.silu(g.astype(jnp.float32)).reshape(B, S, HGRN_HEADS, HGRN_DIM)
    return (o * gate).reshape(B, S, HALF).astype(g.dtype)


def forgetting_attention(q, k, v, f_logit, b_f):
    B, S, _ = q.shape
    q = q.reshape(B, S, FOX_HEADS, FOX_DIM)
    k = k.reshape(B, S, FOX_HEADS, FOX_DIM)
    v = v.reshape(B, S, FOX_HEADS, FOX_DIM)
    log_f = jax.nn.log_sigmoid(f_logit.astype(jnp.float32) + b_f.astype(jnp.float32))
    c = jnp.cumsum(log_f, axis=1).transpose(0, 2, 1)
    scale = FOX_DIM ** -0.5
    outs = []
    for blk in range(S // Q_BLOCK):
        q0, q1 = blk * Q_BLOCK, (blk + 1) * Q_BLOCK
        logits = jnp.einsum('bqhd,bkhd->bhqk', q[:, q0:q1], k[:, :q1]).astype(jnp.float32) * scale
        logits = logits + c[:, :, q0:q1, None] - c[:, :, None, :q1]
        qpos = jnp.arange(q0, q1)[:, None]
        kpos = jnp.arange(q1)[None, :]
        logits = jnp.where(kpos <= qpos, logits, -jnp.inf)
        p = jax.nn.softmax(logits, axis=-1)
        outs.append(jnp.einsum('bhqk,bkhd->bqhd', p.astype(v.dtype), v[:, :q1]))
    return jnp.concatenate(outs, axis=1).reshape(B, S, HALF)


def stick_breaking_attention(q, k, v):
    B, S, _ = q.shape
    q = q.reshape(B, S, SB_HEADS, SB_DIM)
    k = k.reshape(B, S, SB_HEADS, SB_DIM)
    v = v.reshape(B, S, SB_HEADS, SB_DIM)
    scale = SB_DIM ** -0.5
    outs = []
    for blk in range(S // Q_BLOCK):
        q0, q1 = blk * Q_BLOCK, (blk + 1) * Q_BLOCK
        z = jnp.einsum('bqhd,bkhd->bhqk', q[:, q0:q1], k[:, :q1]).astype(jnp.float32) * scale
        qpos = jnp.arange(q0, q1)[:, None]
        kpos = jnp.arange(q1)[None, :]
        mask = kpos < qpos
        log_one_minus = jnp.where(mask, jax.nn.log_sigmoid(-z), 0.0)
        suffix = lax.cumsum(log_one_minus, axis=3, reverse=True) - log_one_minus
        log_w = jax.nn.log_sigmoid(z) + suffix
        w = jnp.where(mask, jnp.exp(log_w), 0.0)
        outs.append(jnp.einsum('bhqk,bkhd->bqhd', w.astype(v.dtype), v[:, :q1]))
    return jnp.concatenate(outs, axis=1).reshape(B, S, HALF)


def rg_lru_branch(xr, gate, conv_w, conv_b, wa, ba, wx, bx, lam):
    B, S, W = xr.shape
    xpad = jnp.pad(xr, ((0, 0), (CONV_WIDTH - 1, 0), (0, 0)))
    xc = conv_b + sum(conv_w[j] * xpad[:, j:j + S] for j in range(CONV_WIDTH))
    xb = xc.reshape(B, S, LRU_BLOCKS, LRU_BLOCK_DIM)
    r = jax.nn.sigmoid(jnp.einsum('bsni,nij->bsnj', xb, wa).reshape(B, S, W) + ba)
    i = jax.nn.sigmoid(jnp.einsum('bsni,nij->bsnj', xb, wx).reshape(B, S, W) + bx)
    log_a = -LRU_C * r.astype(jnp.float32) * jax.nn.softplus(-lam.astype(jnp.float32))
    a = jnp.exp(log_a)
    u = jnp.sqrt(-jnp.expm1(2.0 * log_a)) * (i * xc).astype(jnp.float32)

    def combine(left, right):
        a1, b1 = left
        a2, b2 = right
        return a1 * a2, a2 * b1 + b2

    _, h = lax.associative_scan(combine, (a, u), axis=1)
    return (h * jax.nn.gelu(gate.astype(jnp.float32), approximate=True)).astype(xr.dtype)


def swiglu(x, w13, w2):
    a, b = jnp.split(x @ w13, 2, axis=-1)
    return (jax.nn.silu(a) * b) @ w2


def setup_inputs(seed: int = 0) -> dict:
    key = jax.random.key(seed)
    ks = jax.random.split(key, 24)
    f32 = jnp.float32
    nrm = lambda k, shp, s: jax.random.normal(k, shp, f32) * s
    x = jax.random.normal(ks[0], (BATCH, SEQ, D_MODEL), f32)
    ev_w_in = nrm(ks[1], (N_EVEN, D_MODEL, EVEN_IN), D_MODEL ** -0.5)
    ev_fox_bf = 1.0 + nrm(ks[2], (N_EVEN, FOX_HEADS), 0.1)
    hgrn_lb = nrm(ks[3], (DEPTH + 1, HALF), 0.1)
    ev_hgrn_norm_g = 1.0 + nrm(ks[4], (N_EVEN, HGRN_DIM), 0.02)
    ev_w_out = nrm(ks[5], (N_EVEN, D_MODEL, D_MODEL), BETA * D_MODEL ** -0.5)
    od_w_in = nrm(ks[6], (N_ODD, D_MODEL, ODD_IN), D_MODEL ** -0.5)
    od_conv_w = nrm(ks[7], (N_ODD, CONV_WIDTH, LRU_WIDTH), CONV_WIDTH ** -0.5)
    od_conv_b = nrm(ks[8], (N_ODD, LRU_WIDTH), 0.02)
    od_gate_a_w = nrm(ks[9], (N_ODD, LRU_BLOCKS, LRU_BLOCK_DIM, LRU_BLOCK_DIM), LRU_BLOCK_DIM ** -0.5)
    od_gate_a_b = nrm(ks[10], (N_ODD, LRU_WIDTH), 0.02)
    od_gate_x_w = nrm(ks[11], (N_ODD, LRU_BLOCKS, LRU_BLOCK_DIM, LRU_BLOCK_DIM), LRU_BLOCK_DIM ** -0.5)
    od_gate_x_b = nrm(ks[12], (N_ODD, LRU_WIDTH), 0.02)
    a_c = jax.random.uniform(ks[13], (N_ODD, LRU_WIDTH), f32, 0.9, 0.999) ** (1.0 / LRU_C)
    od_lru_lambda = jnp.log(a_c) - jnp.log1p(-a_c)
    od_w_out = nrm(ks[14], (N_ODD, D_MODEL, D_MODEL), BETA * D_MODEL ** -0.5)
    ffn_w13 = nrm(ks[15], (DEPTH, D_MODEL, 2 * D_FF), D_MODEL ** -0.5)
    ffn_w2 = nrm(ks[16], (DEPTH, D_FF, D_MODEL), BETA * D_FF ** -0.5)
    ln_g = 1.0 + nrm(ks[17], (DEPTH, 2, D_MODEL), 0.02)
    ln_b = nrm(ks[18], (DEPTH, 2, D_MODEL), 0.02)
    return {'x': x, 'ev_w_in': ev_w_in, 'ev_fox_bf': ev_fox_bf, 'hgrn_lb': hgrn_lb,
            'ev_hgrn_norm_g': ev_hgrn_norm_g, 'ev_w_out': ev_w_out, 'od_w_in': od_w_in,
            'od_conv_w': od_conv_w, 'od_conv_b': od_conv_b, 'od_gate_a_w': od_gate_a_w,
            'od_gate_a_b': od_gate_a_b, 'od_gate_x_w': od_gate_x_w, 'od_gate_x_b': od_gate_x_b,
            'od_lru_lambda': od_lru_lambda, 'od_w_out': od_w_out, 'ffn_w13': ffn_w13,
            'ffn_w2': ffn_w2, 'ln_g': ln_g, 'ln_b': ln_b}


def reference(x, ev_w_in, ev_fox_bf, hgrn_lb, ev_hgrn_norm_g, ev_w_out, od_w_in, od_conv_w,
              od_conv_b, od_gate_a_w, od_gate_a_b, od_gate_x_w, od_gate_x_b, od_lru_lambda,
              od_w_out, ffn_w13, ffn_w2, ln_g, ln_b):
    lb_all = jnp.cumsum(jax.nn.softmax(hgrn_lb.astype(jnp.float32), axis=0), axis=0)
    for layer in range(DEPTH):
        if layer % 2 == 0:
            e = layer // 2
            proj = x @ ev_w_in[e]
            qa, fa, ia, ga, qb, kb, vb, fb = split_cols(proj, [HALF] * 7 + [FOX_HEADS])
            ya = hgrn2(qa, fa, ia, ga, lb_all[layer], ev_hgrn_norm_g[e])
            yb = forgetting_attention(qb, kb, vb, fb, ev_fox_bf[e])
            mix = jnp.concatenate([ya, yb], axis=-1) @ ev_w_out[e]
        else:
            o = layer // 2
            proj = x @ od_w_in[o]
            xr, gr, qd, kd, vd = split_cols(proj, [LRU_WIDTH, LRU_WIDTH, HALF, HALF, HALF])
            yc = rg_lru_branch(xr, gr, od_conv_w[o], od_conv_b[o], od_gate_a_w[o], od_gate_a_b[o],
                               od_gate_x_w[o], od_gate_x_b[o], od_lru_lambda[o])
            yd = stick_breaking_attention(qd, kd, vd)
            mix = jnp.concatenate([yc, yd], axis=-1) @ od_w_out[o]
        x = layer_norm(ALPHA * x + mix, ln_g[layer, 0], ln_b[layer, 0])
        x = layer_norm(ALPHA * x + swiglu(x, ffn_w13[layer], ffn_w2[layer]), ln_g[layer, 1], ln_b[layer, 1])
    return x
```

```python
import contextlib
import math
import numpy as np
import concourse.bass as bass
import concourse.mybir as mybir
from concourse.bass_utils import run_bass_kernel_spmd

F32 = mybir.dt.float32
BF16 = mybir.dt.bfloat16
AF = mybir.ActivationFunctionType
ALU = mybir.AluOpType
AX = mybir.AxisListType

ENGS = ("pe", "act", "dve", "pool", "sp")
INLINE_WAIT = True
S_TOK = 2048
D = 1024
NT = 16
ALPHA = 4.0 ** 0.25
EPS = 1e-5
DFF = 2816
NF = 22


def _geom(ap):
    t = ap.tensor
    dims = ap.ap
    off = int(ap.offset)
    space = str(ap.space)
    if "PSUM" in space:
        return (t.name, 0, 128, 0, 1 << 30, True)
    if "SB" in space:
        pstep, pcnt = dims[0]
        if pstep == 0:
            p0, f0 = 0, off
        else:
            p0, f0 = off // pstep, off % pstep
        rest = dims[1:]
        p1 = p0 + pcnt
    else:
        p0, p1 = 0, 1
        f0 = off
        rest = dims
    ext = 1
    lo = 0
    for st, cnt in rest:
        if cnt > 1:
            if st >= 0:
                ext += st * (cnt - 1)
            else:
                lo += st * (cnt - 1)
    return (t.name, p0, p1, f0 + lo, f0 + ext, False)


class Op:
    __slots__ = ("eng", "fn", "waits", "is_dma", "sem", "val", "needed", "waited")


class Sched:
    def __init__(self, nc):
        self.nc = nc
        self.ops = {e: [] for e in ENGS}
        self.acc = {}
        self.dma_cnt = {}

    def _deps(self, op, reads, writes):
        deps = []
        for ap, is_w in [(a, False) for a in reads] + [(a, True) for a in writes]:
            name, p0, p1, f0, f1, excl = _geom(ap)
            if excl:
                is_w = True
            lst = self.acc.get(name, [])
            new = []
            for ent in lst:
                q0, q1, g0, g1, o, w = ent
                overlap = not (p1 <= q0 or q1 <= p0 or f1 <= g0 or g1 <= f0)
                if overlap and (is_w or w) and o is not op:
                    deps.append((o, w and not excl, is_w and not excl))
                if is_w and overlap and p0 <= q0 and q1 <= p1 and f0 <= g0 and g1 <= f1:
                    continue
                new.append(ent)
            new.append([p0, p1, f0, f1, op, is_w])
            self.acc[name] = new
        return deps

    def _add(self, eng, fn, reads, writes, is_dma=False, tag=None):
        op = Op()
        op.eng = eng
        op.fn = fn
        op.is_dma = is_dma
        op.needed = is_dma
        op.waited = False
        op.waits = []
        if is_dma:
            c = self.dma_cnt.get(tag, 0) + 16
            self.dma_cnt[tag] = c
            op.sem = ("dma", tag)
            op.val = c
        else:
            op.sem = ("eng", eng)
            op.val = None
        seen = set()
        for o, o_w, me_w in self._deps(op, reads, writes):
            if id(o) in seen:
                continue
            if not o.is_dma and not is_dma and o.eng == eng:
                if eng == "pe":
                    continue
                if eng != "pool" and not (o_w and not me_w):
                    continue
            seen.add(id(o))
            o.needed = True
            o.waited = True
            op.waits.append(o)
        self.ops[eng].append(op)
        return op

    def op(self, eng, fn, reads=(), writes=()):
        return self._add(eng, fn, list(reads), list(writes))

    def dma(self, eng, out, in_, tag=None, **kw):
        if tag is None:
            sb = out if "DRAM" not in str(out.space).upper() else in_
            tag = _geom(sb)[:4]
        return self._add(eng, lambda e: e.dma_start(out=out, in_=in_, **kw), [in_], [out],
                         is_dma=True, tag=tag)

    def emit(self):
        nc = self.nc
        for e in ENGS:
            c = 0
            for o in self.ops[e]:
                if not o.is_dma and o.needed:
                    c += 1
                    o.val = c
        sem_keys = [("eng", e) for e in ENGS if any((not o.is_dma) and o.needed for o in self.ops[e])]
        sem_keys += [("dma", t) for t in self.dma_cnt]
        assert len(sem_keys) < 200, len(sem_keys)
        sems = {}
        for i, k in enumerate(sem_keys):
            _UID[0] += 1
            sems[k] = nc.alloc_semaphore(name="s%d_%d" % (i, _UID[0]))
        with contextlib.ExitStack() as st:
            block = st.enter_context(nc.Block())

            def run(e, eobj):
                seen = {}
                for o in self.ops[e]:
                    need = {}
                    for d in o.waits:
                        if d.val > need.get(d.sem, 0):
                            need[d.sem] = d.val
                    pend_w = []
                    for k_, v_ in need.items():
                        if seen.get(k_, 0) >= v_:
                            continue
                        seen[k_] = v_
                        pend_w.append((k_, v_))
                    inline = INLINE_WAIT and (not o.is_dma) and len(pend_w) >= 1
                    for k_, v_ in (pend_w[:-1] if inline else pend_w):
                        eobj.wait_ge(sems[k_], v_)
                    ins = o.fn(eobj)
                    if inline:
                        ins._wait_ge(sems[pend_w[-1][0]], pend_w[-1][1])
                    if o.is_dma:
                        ins.then_inc(sems[o.sem], 16)
                    elif o.needed:
                        ins.then_inc(sems[o.sem], 1)
                last = {}
                for o in self.ops[e]:
                    if o.is_dma:
                        last[o.sem] = o.val
                for k, v in last.items():
                    if seen.get(k, 0) < v:
                        eobj.wait_ge(sems[k], v)

            if self.ops["pe"]:
                block.tensor(lambda t: run("pe", t))
            if self.ops["act"]:
                block.scalar(lambda t: run("act", t))
            if self.ops["dve"]:
                block.vector(lambda t: run("dve", t))
            if self.ops["pool"]:
                block.gpsimd(lambda t: run("pool", t))
            if self.ops["sp"]:
                block.sync(lambda t: run("sp", t))
        nc.all_engine_barrier()
        nc.clear_and_free_semaphores(list(sems.values()))
        nc.all_engine_barrier()
        return {e: len(self.ops[e]) for e in ENGS}

    def mm(self, out, lhsT, rhs, start, stop):
        return self.op("pe", lambda e: e.matmul(out, lhsT=lhsT, rhs=rhs, start=start, stop=stop),
                       reads=[lhsT, rhs], writes=[out])

    def tr(self, out, in_, ident):
        return self.op("pe", lambda e: e.transpose(out=out, in_=in_, identity=ident),
                       reads=[in_, ident], writes=[out])

    def act(self, out, in_, func, bias=None, scale=None, accum_out=None):
        kw = {}
        rd = [in_]
        wr = [out]
        if bias is not None:
            kw["bias"] = bias
            if not isinstance(bias, (int, float)):
                rd.append(bias)
        if scale is not None:
            kw["scale"] = scale
            if not isinstance(scale, (int, float)):
                rd.append(scale)
        if accum_out is not None:
            kw["accum_out"] = accum_out
            wr.append(accum_out)
        return self.op("act", lambda e: e.activation(out=out, in_=in_, func=func, **kw), reads=rd, writes=wr)

    def tt(self, eng, out, a, b, op):
        return self.op(eng, lambda e: e.tensor_tensor(out=out, in0=a, in1=b, op=op), reads=[a, b], writes=[out])

    def ts(self, eng, out, a, s1, s2, op0, op1=None):
        rd = [a] + [s for s in (s1, s2) if s is not None and not isinstance(s, (int, float))]
        if op1 is None:
            return self.op(eng, lambda e: e.tensor_scalar(out=out, in0=a, scalar1=s1, scalar2=None, op0=op0),
                           reads=rd, writes=[out])
        return self.op(eng, lambda e: e.tensor_scalar(out=out, in0=a, scalar1=s1, scalar2=s2, op0=op0, op1=op1),
                       reads=rd, writes=[out])

    def stt(self, out, in0, scalar, in1, op0, op1, eng="dve"):
        rd = [in0, in1] + ([] if isinstance(scalar, (int, float)) else [scalar])
        return self.op(eng, lambda e: e.scalar_tensor_tensor(out=out, in0=in0, scalar=scalar, in1=in1, op0=op0, op1=op1),
                       reads=rd, writes=[out])

    def cp(self, eng, out, in_):
        if eng == "act":
            return self.op("act", lambda e: e.copy(out=out, in_=in_), reads=[in_], writes=[out])
        return self.op(eng, lambda e: e.tensor_copy(out=out, in_=in_), reads=[in_], writes=[out])

    def scan(self, out, d0, d1, initial, op0=ALU.mult, op1=ALU.add):
        rd = [d0, d1] + ([] if isinstance(initial, (int, float)) else [initial])
        return self.op("dve", lambda e: e.tensor_tensor_scan(out=out, data0=d0, data1=d1, initial=initial, op0=op0, op1=op1),
                       reads=rd, writes=[out])

    def memset(self, eng, ap, v):
        return self.op(eng, lambda e: e.memset(ap, v), writes=[ap])

    def recip(self, out, in_):
        return self.op("dve", lambda e: e.reciprocal(out=out, in_=in_), reads=[in_], writes=[out])

    def aselect(self, out, in_, pattern, cmp, fill, base, cm):
        return self.op("pool", lambda e: e.affine_select(out=out, in_=in_, pattern=pattern, compare_op=cmp,
                                                         fill=fill, base=base, channel_multiplier=cm),
                       reads=[in_], writes=[out])


class Ctx:
    pass


_UID = [0]


def mkT(nc, st):
    def T(name, shape, dt):
        _UID[0] += 1
        return st.enter_context(nc.sbuf_tensor("%s_%d" % (name, _UID[0]), shape, dt))
    return T


def wload(S, dst, w, c0, n, parts=4):
    v = wview(w, c0, n)
    step = 8 // parts
    for i in range(parts):
        S.dma("pool", dst[:, i * step:(i + 1) * step, :], v[:, i * step:(i + 1) * step, :])


def wview(w, c0, n):
    return w[:, c0:c0 + n].rearrange("(c p) n -> p c n", p=128)


def phase_load(nc, C, x, S=None):
    own = S is None
    if own:
        S = Sched(nc)
    xv = x.rearrange("(t p) d -> p t d", p=128)
    for i in range(4):
        S.dma("sp", C.res[:, 4 * i:4 * i + 4, :], xv[:, 4 * i:4 * i + 4, :])
    S.memset("pool", C.ident_f[:], 1.0)
    S.aselect(C.ident_f[:], C.ident_f[:], [[-1, 128]], ALU.is_equal, 0.0, 0, 1)
    S.cp("pool", C.ident_b[:], C.ident_f[:])
    S.memset("pool", C.ones_f[:], 1.0)
    for t in range(NT):
        transpose_to_xT(S, C, t)
    if own:
        S.emit()


def transpose_to_xT(S, C, t):
    for half in range(2):
        pt = C.P[6 + half]
        for j in range(4):
            c = half * 4 + j
            S.tr(pt[:, j * 128:(j + 1) * 128], C.res[:, t, c * 128:(c + 1) * 128], C.ident_f[:])
        S.cp("act", C.xT[:, half * 4:half * 4 + 4, t * 128:(t + 1) * 128],
             pt[:].rearrange("p (c n) -> p c n", c=4))


def ln_stats(S, C, L, t):
    st = L.stats[:, t % 2]
    S.op("dve", lambda e: e.bn_stats(out=st[:, 0, :], in_=C.res[:, t, 0:512]), reads=[C.res[:, t, 0:512]], writes=[st[:, 0, :]])
    S.op("dve", lambda e: e.bn_stats(out=st[:, 1, :], in_=C.res[:, t, 512:1024]), reads=[C.res[:, t, 512:1024]], writes=[st[:, 1, :]])
    mv = L.mv[:, t % 2, :]
    S.op("dve", lambda e: e.bn_aggr(out=mv, in_=st.rearrange("p a b -> p (a b)")), reads=[st], writes=[mv])
    sd = L.sd[:, t % 2, :]
    S.act(sd, mv[:, 1:2], AF.Sqrt, bias=L.eps[:, 0:1], scale=1.0)


def ln_apply(S, C, L, t, to_xT=True, y=None, eng="pool"):
    r = C.res[:, t, :]
    mv = L.mv[:, t % 2, :]
    sd = L.sd[:, t % 2, :]
    nb = L.nb[:, t % 2, :]
    S.recip(sd, sd)
    S.stt(nb, mv[:, 0:1], -1.0, sd, ALU.mult, ALU.mult)
    tmp = L.tmp[:, t % 2, :]
    S.act(tmp, r, AF.Identity, bias=nb, scale=sd)
    S.tt("dve", tmp, tmp, L.g[:], ALU.mult)
    S.tt("pool", r, tmp, L.b[:], ALU.add)
    if y is not None:
        S.dma("sp", y[t * 128:(t + 1) * 128, :], C.res[:, t, :])
    elif to_xT:
        xb = L.xb[:, t % 2, :]
        S.tt("dve", xb, tmp, L.b[:], ALU.add)
        dst = C.xT[:, :, t * 128:(t + 1) * 128]
        S._add("sp", lambda e: e.dma_start_transpose(out=dst, in_=xb), [xb], [dst], is_dma=True, tag=("xTtr", t % 2))


def ln_tile(S, C, L, t, to_xT=True, y=None, eng="pool"):
    ln_stats(S, C, L, t)
    ln_apply(S, C, L, t, to_xT, y, eng)


def alloc_ln(nc, st, C, S, ln_g, ln_b, layer, which):
    L = Ctx()
    T = mkT(nc, st)
    L.g = T("ln_g%d%d" % (layer, which), [128, 1024], F32)
    L.b = T("ln_b%d%d" % (layer, which), [128, 1024], F32)
    L.stats = T("ln_st%d%d" % (layer, which), [128, 2, 2, 6], F32)
    L.mv = T("ln_mv%d%d" % (layer, which), [128, 2, 2], F32)
    L.sd = T("ln_sd%d%d" % (layer, which), [128, 2, 1], F32)
    L.nb = T("ln_nb%d%d" % (layer, which), [128, 2, 1], F32)
    L.eps = T("ln_eps%d%d" % (layer, which), [128, 1], F32)
    L.tmp = T("ln_tmp%d%d" % (layer, which), [128, 2, 1024], F32)
    L.xb = T("ln_xb%d%d" % (layer, which), [128, 2, 1024], BF16)
    S.dma("sp", L.g[:], ln_g[layer, which:which + 1, :].partition_broadcast(128))
    S.dma("sp", L.b[:], ln_b[layer, which:which + 1, :].partition_broadcast(128))
    S.memset("pool", L.eps[:], EPS)
    return L


def outproj_partial(S, C, yT_fn, nchunk, wo, first):
    k = 0
    for t in range(NT):
        for half in range(2):
            pb = C.P[4 + (k % 2)]
            k += 1
            for c in range(nchunk):
                S.mm(pb[:, :], yT_fn(t, c), wo[:, c, half * 512:(half + 1) * 512], c == 0, c == nchunk - 1)
            r = C.res[:, t, half * 512:(half + 1) * 512]
            if first:
                S.stt(r, r, ALPHA, pb[:, :], ALU.mult, ALU.add)
            else:
                S.tt("dve", r, r, pb[:, :], ALU.add)


def post_outproj_ln(S, C, L, ytm, wo, ytile, dbg=None):
    def stage1(t):
        q = C.Q[t % 2]
        for c in range(4):
            S.tr(q[:, c * 128:(c + 1) * 128], ytm[:, t, c * 128:(c + 1) * 128], C.ident_b[:])
        S.cp("act", ytile[:, t % 2, :, :], q[:, 0:512].rearrange("p (c n) -> p c n", c=4))
        if dbg is not None:
            S.dma("sp", dbg[t * 128:(t + 1) * 128, :], ytm[:, t, :])
        for half in range(2):
            pb = C.P[2 * (t % 2) + half]
            for c in range(4):
                S.mm(pb[:, :], ytile[:, t % 2, c, :], wo[:, c, half * 512:(half + 1) * 512], c == 0, c == 3)
            r = C.res[:, t, half * 512:(half + 1) * 512]
            S.tt("dve", r, r, pb[:, :], ALU.add)

    for t in range(NT + 2):
        if t < NT:
            stage1(t)
        if L is not None and 1 <= t <= NT:
            ln_stats(S, C, L, t - 1)
        if L is not None and t >= 2:
            ln_apply(S, C, L, t - 2)


def phase_hgrn_pair(nc, C, W, hp, dbg=None, x=None):
    w_in, w_out = W["ev_w_in"], W["ev_w_out"]
    with contextlib.ExitStack() as st:
        T = mkT(nc, st)
        S = Sched(nc)
        if x is not None:
            phase_load(nc, C, x, S)
        wq = T("h_wq", [128, 8, 256], BF16)
        wf = T("h_wf", [128, 8, 256], BF16)
        wv = T("h_wv", [128, 8, 256], BF16)
        wg = T("h_wg", [128, 8, 256], BF16)
        wo = T("h_wo", [128, 2, 1024], BF16)
        qt = T("h_qt", [128, 2, S_TOK], BF16)
        kt = T("h_kt", [128, 2, S_TOK], BF16)
        ktm = T("h_ktm", [128, 2, NT, 128], BF16)
        vtm = T("h_vtm", [128, NT, 256], BF16)
        gate = T("h_gate", [128, NT, 256], BF16)
        yTp = T("h_yT", [128, 2, S_TOK], BF16)
        ebl = T("h_ebl", [128, 2, 32], F32)
        lb3 = T("h_lb3", [128, 3, 4], F32)
        lbs = T("h_lbs", [128, 4, 4], F32)
        ngb = T("h_ngb", [128, 256], F32)
        tsig = T("h_sig", [128, 3, 512], F32)
        tlf = T("h_lf", [128, 3, 512], F32)
        tkk = T("h_kk", [128, 3, 512], F32)
        te = T("h_e", [128, 3, 512], F32)
        te2 = T("h_e2", [128, 3, 512], F32)
        tsg = T("h_sg", [128, 2, 256], F32)
        stf = T("h_stf", [128, 2, 128], F32)
        stb = T("h_stb", [128, 2, 128], BF16)
        stmp = T("h_stmp", [128, 2, 128], F32)
        scm = T("h_scm", [128, 2, 128], BF16)
        ss = T("h_ss", [128, 2, 2], F32)
        rstd = T("h_rstd", [128, 2, 2], F32)
        junk = T("h_junk", [128, 128], F32)
        yat = T("h_yat", [128, 2, 256], BF16)
        epsb = T("h_eps", [128, 1], F32)
        C.hmask = T("hmask", [128, 2, 128], F32)
        C.rmask = T("rmask", [128, 512], F32)
        S.memset("pool", C.hmask[:], 1.0)
        for hh_ in range(2):
            v = C.hmask[:, hh_, :]
            S.aselect(v, v, [[1, 128]], ALU.is_ge, 0.0, 0, -1)
        S.memset("pool", C.rmask[:], 1.0)
        S.memset("pool", C.rmask[:].rearrange("p (c k) -> p c k", k=128)[:, :, 0:1], 0.0)

        c0 = hp * 256
        wload(S, wq, w_in, 0 + c0, 256)
        wload(S, wf, w_in, 512 + c0, 256)
        S.dma("pool", wv[:], wview(w_in, 1024 + c0, 256))
        S.dma("pool", wg[:], wview(w_in, 1536 + c0, 256))
        S.dma("pool", wo[:], w_out[c0:c0 + 256, :].rearrange("(c p) n -> p c n", p=128))
        S.dma("sp", lb3[:], W["lb3"])
        S.dma("sp", ngb[:], W["normg"].partition_broadcast(128))
        S.memset("pool", epsb[:], EPS)
        S.act(lb3[:], lb3[:], AF.Exp)
        S.tt("dve", lbs[:, 3, :], lb3[:, 0, :], lb3[:, 1, :], ALU.add)
        S.tt("dve", lbs[:, 3, :], lbs[:, 3, :], lb3[:, 2, :], ALU.add)
        S.recip(lbs[:, 3, :], lbs[:, 3, :])
        S.tt("dve", lbs[:, 0, :], lb3[:, 0, :], lbs[:, 3, :], ALU.mult)
        S.ts("dve", lbs[:, 1, :], lbs[:, 0, :], -1.0, 1.0, ALU.mult, ALU.add)
        S.ts("dve", lbs[:, 2, :], lbs[:, 1, :], -1.0, None, ALU.mult)

        def h1_p(k, hh, g):
            h = 2 * hp + hh
            gc = slice(g * 512, (g + 1) * 512)
            pq = C.P[0 + (k % 3)]
            pf = C.P[3 + (k % 3)]
            b = k % 3
            for c in range(8):
                S.mm(pq[:, :], wq[:, c, hh * 128:(hh + 1) * 128], C.xT[:, c, gc], c == 0, c == 7)
            for c in range(8):
                S.mm(pf[:, :], wf[:, c, hh * 128:(hh + 1) * 128], C.xT[:, c, gc], c == 0, c == 7)
            S.act(tsig[:, b, :], pf[:, :], AF.Sigmoid)
            S.act(tlf[:, b, :], tsig[:, b, :], AF.Ln, bias=lbs[:, 0, h:h + 1], scale=lbs[:, 1, h:h + 1])
            S.ts("pool", tkk[:, b, :], tsig[:, b, :], lbs[:, 2, h:h + 1], lbs[:, 1, h:h + 1], ALU.mult, ALU.add)
            S.scan(tlf[:, b, :], C.rmask[:], tlf[:, b, :], 0.0)

        def h1_e(k, hh, g):
            gc = slice(g * 512, (g + 1) * 512)
            pq = C.P[0 + (k % 3)]
            b = k % 3
            S.act(te[:, b, :], tlf[:, b, :], AF.Exp)
            S.tt("dve", qt[:, hh, gc], pq[:, :], te[:, b, :], ALU.mult)
            S.cp("pool", ebl[:, hh, g * 4:(g + 1) * 4], te[:, b, :].rearrange("p (c k) -> p c k", k=128)[:, :, 127])
            S.act(te2[:, b, :], tlf[:, b, :], AF.Exp, scale=-1.0)
            S.tt("pool", kt[:, hh, gc], tkk[:, b, :], te2[:, b, :], ALU.mult)

        items = [(hh, g) for hh in range(2) for g in range(4)]
        for k, (hh, g) in enumerate(items):
            h1_p(k, hh, g)
            if k >= 2:
                h1_e(k - 2, *items[k - 2])
        h1_e(len(items) - 2, *items[-2])
        h1_e(len(items) - 1, *items[-1])
        for hh in range(2):
            for tb in range(2):
                q = C.Q[tb % 2]
                for j in range(8):
                    t = tb * 8 + j
                    S.tr(q[:, j * 128:(j + 1) * 128], kt[:, hh, t * 128:(t + 1) * 128], C.ident_b[:])
                S.cp("act", ktm[:, hh, tb * 8:(tb + 1) * 8, :], q[:].rearrange("p (t n) -> p t n", t=8))
        for t in range(NT):
            pb = C.P[t % 2]
            tc_ = slice(t * 128, (t + 1) * 128)
            for c in range(8):
                S.mm(pb[:, 0:256], C.xT[:, c, tc_], wv[:, c, :], c == 0, c == 7)
            for c in range(8):
                S.mm(pb[:, 256:512], C.xT[:, c, tc_], wg[:, c, :], c == 0, c == 7)
            S.cp("act", vtm[:, t, :], pb[:, 0:256])
            S.act(tsg[:, t % 2, :], pb[:, 256:512], AF.Silu)
            S.tt("pool", gate[:, t, :], tsg[:, t % 2, :], ngb[:], ALU.mult)

        S.memset("pool", stf[:], 0.0)
        S.memset("pool", stb[:], 0.0)

        def h2_r(t):
            tc_ = slice(t * 128, (t + 1) * 128)
            psc = C.P[0]
            ob = 1 if t % 2 == 0 else 4
            for hh in range(2):
                S.mm(psc[:, hh * 128:(hh + 1) * 128], kt[:, hh, tc_], qt[:, hh, tc_], True, True)
            S.tt("dve", scm[:].rearrange("p a b -> p (a b)"), psc[:, 0:256],
                 C.hmask[:].rearrange("p a b -> p (a b)"), ALU.mult)
            for hh in range(2):
                po = C.P[ob + hh]
                S.mm(po[:, 0:128], scm[:, hh, :], vtm[:, t, hh * 128:(hh + 1) * 128], True, False)
                S.mm(po[:, 0:128], qt[:, hh, tc_], stb[:, hh, :], False, True)
            pd = C.P[3]
            ecb = ebl[:, :, t:t + 1].to_broadcast([128, 2, 128])
            for hh in range(2):
                S.mm(pd[:, hh * 128:(hh + 1) * 128], ktm[:, hh, t, :], vtm[:, t, hh * 128:(hh + 1) * 128], True, True)
            S.op("dve", lambda e: e.tensor_tensor(out=stmp[:], in0=pd[:, 0:256].rearrange("p (a b) -> p a b", a=2),
                                                 in1=stf[:], op=ALU.add),
                 reads=[pd[:, 0:256], stf[:]], writes=[stmp[:]])
            S.op("dve", lambda e, ecb=ecb: e.tensor_tensor(out=stb[:], in0=stmp[:], in1=ecb, op=ALU.mult),
                 reads=[stmp[:], ebl[:, :, t:t + 1]], writes=[stb[:]])
            S.op("dve", lambda e, ecb=ecb: e.tensor_tensor(out=stf[:], in0=stmp[:], in1=ecb, op=ALU.mult),
                 reads=[stmp[:], ebl[:, :, t:t + 1]], writes=[stf[:]])

        def h2_o(t):
            tc_ = slice(t * 128, (t + 1) * 128)
            ob = 1 if t % 2 == 0 else 4
            for hh in range(2):
                po = C.P[ob + hh]
                S.act(junk[:], po[:, 0:128], AF.Square, accum_out=ss[:, t % 2, hh:hh + 1])
            S.act(rstd[:, t % 2, :], ss[:, t % 2, :], AF.Ln, bias=epsb[:, 0:1], scale=1.0 / 128.0)
            S.act(rstd[:, t % 2, :], rstd[:, t % 2, :], AF.Exp, scale=-0.5)
            for hh in range(2):
                po = C.P[ob + hh]
                S.stt(yat[:, t % 2, hh * 128:(hh + 1) * 128], po[:, 0:128], rstd[:, t % 2, hh:hh + 1],
                      gate[:, t, hh * 128:(hh + 1) * 128], ALU.mult, ALU.mult)
            q = C.Q[t % 2]
            for hh in range(2):
                S.tr(q[:, hh * 128:(hh + 1) * 128], yat[:, t % 2, hh * 128:(hh + 1) * 128], C.ident_b[:])
            S.cp("act", yTp[:, :, tc_], q[:, 0:256].rearrange("p (a n) -> p a n", a=2))
            if dbg is not None:
                S.dma("sp", dbg[t * 128:(t + 1) * 128, c0:c0 + 256], yat[:, t % 2, :])

        for t in range(NT):
            h2_r(t)
            if t >= 1:
                h2_o(t - 1)
        h2_o(NT - 1)
        outproj_partial(S, C, lambda t, c: yTp[:, c, t * 128:(t + 1) * 128], 2, wo, first=(hp == 0))
        S.emit()


def phase_fox(nc, C, W, dbg=None):
    w_in, w_out = W["ev_w_in"], W["ev_w_out"]
    SCALE = 0.125
    with contextlib.ExitStack() as st:
        T = mkT(nc, st)
        vb = T("f_vb", [128, NT * 8, 65], BF16)
        ybtm = T("f_ybtm", [128, NT, 512], BF16)
        wo = T("f_wo", [128, 4, 1024], BF16)
        crow = T("f_crow", [8, S_TOK], BF16)
        cnegT = T("f_cnegT", [128, NT, 8], F32)
        negbf = T("f_negbf", [8, 1], F32)
        st1 = st.enter_context(contextlib.ExitStack())
        T1 = mkT(nc, st1)
        S = Sched(nc)
        wv = T1("f_wv", [128, 8, 512], BF16)
        wfb = T1("f_wfb", [128, 8, 8], BF16)
        cneg = T1("f_cneg", [8, S_TOK], F32)
        sp8 = cneg
        e8 = T1("f_e8", [8, 2, 512], F32)
        S.dma("pool", wfb[:], wview(w_in, 3584, 8))
        wload(S, wv, w_in, 3072, 512)
        S.dma("sp", negbf[:], W["fox_bf"])
        S.ts("dve", negbf[:], negbf[:], -1.0, None, ALU.mult)
        S.memset("pool", vb[:, :, 64:65], 1.0)
        for g in range(4):
            gc = slice(g * 512, (g + 1) * 512)
            pb = C.P[g % 2]
            for c in range(8):
                S.mm(pb[0:8, :], wfb[:, c, :], C.xT[:, c, gc], c == 0, c == 7)
            S.act(e8[:, g % 2, :], pb[0:8, :], AF.Exp, bias=negbf[:, 0:1], scale=-1.0)
            S.act(sp8[:, gc], e8[:, g % 2, :], AF.Ln, bias=1.0, scale=1.0)
        for g in range(4):
            gc = slice(g * 512, (g + 1) * 512)
            S.scan(cneg[:, gc], C.ones_f[0:8, :], sp8[:, gc], 0.0 if g == 0 else cneg[:, g * 512 - 1:g * 512])
        S.ts("dve", crow[:], cneg[:], -8.0, None, ALU.mult)
        pb = C.P[2]
        for t in range(NT):
            S.tr(pb[:, t * 8:(t + 1) * 8], cneg[0:8, t * 128:(t + 1) * 128], C.ident_f[0:8, 0:8])
        S.cp("act", cnegT[:].rearrange("p t h -> p (t h)"), pb[:, 0:128])
        for t in range(NT):
            pb = C.P[t % 2]
            for c in range(8):
                S.mm(pb[:, :], C.xT[:, c, t * 128:(t + 1) * 128], wv[:, c, :], c == 0, c == 7)
            S.cp("act", vb[:, t * 8:(t + 1) * 8, 0:64], pb[:, :].rearrange("p (h d) -> p h d", h=8))
        S.emit()
        st1.close()
        S = Sched(nc)
        st2 = st.enter_context(contextlib.ExitStack())
        T2 = mkT(nc, st2)
        wq = T2("f_wq", [128, 8, 512], BF16)
        wk = T2("f_wk", [128, 8, 512], BF16)
        qTp = T2("f_qTp", [128, 2, S_TOK], BF16)
        kTp = T2("f_kTp", [128, 2, S_TOK], BF16)
        PT = T2("f_PT", [128, 20, 512], BF16)
        rc = T2("f_rc", [128, 2, 4], F32)
        wload(S, wq, w_in, 2048, 512)
        wload(S, wk, w_in, 2560, 512)
        S.dma("pool", wo[:], w_out[512:1024, :].rearrange("(c p) n -> p c n", p=128))
        S.memset("pool", qTp[:], 0.0)
        S.memset("dve", kTp[:], 0.0)
        S.memset("dve", kTp[64:65, :, :], 1.0)
        ring = [0]
        kk = [0]

        def fx_proj(h):
            if h % 2 == 1:
                return
            for g in range(4):
                gc = slice(g * 512, (g + 1) * 512)
                pq = C.P[kk[0] % 2]
                kk[0] += 1
                for c in range(8):
                    S.mm(pq[:, :], wq[:, c, h * 64:(h + 2) * 64], C.xT[:, c, gc], c == 0, c == 7)
                S.cp("act", qTp[0:64, 0, gc], pq[0:64, :])
                S.cp("act", qTp[0:64, 1, gc], pq[64:128, :])
                pk = C.P[kk[0] % 2]
                kk[0] += 1
                for c in range(8):
                    S.mm(pk[:, :], wk[:, c, h * 64:(h + 2) * 64], C.xT[:, c, gc], c == 0, c == 7)
                S.cp("dve", kTp[0:64, 0, gc], pk[0:64, :])
                S.cp("dve", kTp[0:64, 1, gc], pk[64:128, :])
            S.dma("sp", qTp[64:65, 0, :], crow[h:h + 1, :])
            S.dma("sp", qTp[64:65, 1, :], crow[h + 1:h + 2, :])

        def fx_a(h, g):
            hs = h % 2
            i0 = 4 * g
            slots = {}
            for j in range(i0 + 4):
                base = max(j, i0)
                n = (i0 + 4 - base) * 128
                ps = C.P[2 + (kk[0] % 2)]
                kk[0] += 1
                S.mm(ps[:, 0:n], kTp[:, hs, j * 128:(j + 1) * 128], qTp[:, hs, base * 128:(i0 + 4) * 128], True, True)
                sl = ring[0] % 20
                ring[0] += 1
                S.act(PT[:, sl, 0:n], ps[:, 0:n], AF.Exp, bias=cnegT[:, j, h:h + 1], scale=SCALE)
                if j >= i0:
                    S.aselect(PT[:, sl, 0:128], PT[:, sl, 0:128], [[1, 128]], ALU.is_ge, 0.0, 0, -1)
                slots[j] = (sl, base)
            return slots

        def fx_b(h, g, slots):
            i0 = 4 * g
            po = C.P[4 + (g % 2)]
            for ii in range(4):
                i = i0 + ii
                for j in range(i + 1):
                    sl, base = slots[j]
                    S.mm(po[:, ii * 65:(ii + 1) * 65], PT[:, sl, (i - base) * 128:(i - base + 1) * 128],
                         vb[:, j * 8 + h, :], j == 0, j == i)
            pov = po[:, 0:260].rearrange("p (i d) -> p i d", d=65)
            S.recip(rc[:, g % 2, :], pov[:, :, 64])
            for ii in range(4):
                S.ts("dve", ybtm[:, i0 + ii, h * 64:(h + 1) * 64], po[:, ii * 65:ii * 65 + 64],
                     rc[:, g % 2, ii:ii + 1], None, ALU.mult)

        pend = None
        for h in range(8):
            for g in range(4):
                if g == 0:
                    fx_proj(h)
                need = 4 * g + 4
                if pend is not None and len(pend[2]) + need <= 20:
                    sl = fx_a(h, g)
                    fx_b(*pend)
                else:
                    if pend is not None:
                        fx_b(*pend)
                    sl = fx_a(h, g)
                pend = (h, g, sl)
        fx_b(*pend)
        S.emit()
        st2.close()
        S = Sched(nc)
        ytile = T("f_ytile", [128, 2, 4, 128], BF16)
        L = alloc_ln(nc, st, C, S, W["ln_g"], W["ln_b"], 0, 0) if dbg is None else None
        post_outproj_ln(S, C, L, ybtm, wo, ytile, dbg)
        S.emit()


def phase_ln(nc, C, W, layer, which, y=None):
    with contextlib.ExitStack() as st:
        S = Sched(nc)
        L = alloc_ln(nc, st, C, S, W["ln_g"], W["ln_b"], layer, which)
        for t in range(NT):
            ln_tile(S, C, L, t, to_xT=(y is None))
            if y is not None:
                S.dma("sp", y[t * 128:(t + 1) * 128, :], C.res[:, t, :])
        S.emit()


def phase_ffn(nc, C, W, layer, y=None):
    w13, w2 = W["ffn_w13"], W["ffn_w2"]
    groups = [(0, 6), (6, 12), (12, 18), (18, 22)]
    with contextlib.ExitStack() as st:
        T = mkT(nc, st)
        S = Sched(nc)
        L = alloc_ln(nc, st, C, S, W["ln_g"], W["ln_b"], layer, 1)
        w2b = T("n_w2b", [128, 2, 6, 1024], BF16)
        hT = T("n_hT", [128, 6, S_TOK], BF16)
        wt = T("n_wt", [128, 3, 8, 2, 256], BF16)
        sa = T("n_sa", [128, 2, 512], F32)
        w2v = w2[layer].rearrange("(f p) n -> p f n", p=128)
        def w13_block(wb, f, ff, tb, f0):
            tcs = slice(tb * 512, (tb + 1) * 512)
            pa = C.P[kk[0] % 2]
            pb = C.P[2 + kk[0] % 2]
            sb_ = kk[0] % 2
            kk[0] += 1
            for c in range(8):
                S.mm(pa[:, :], wt[:, wb, c, 0, ff * 128:(ff + 1) * 128], C.xT[:, c, tcs], c == 0, c == 7)
            for c in range(8):
                S.mm(pb[:, :], wt[:, wb, c, 1, ff * 128:(ff + 1) * 128], C.xT[:, c, tcs], c == 0, c == 7)
            S.act(sa[:, sb_, :], pa[:, :], AF.Silu)
            S.tt("dve", hT[:, f - f0, tcs], sa[:, sb_, :], pb[:, :], ALU.mult)

        def w2_tile(gi, ng, t, last):
            for half in range(2):
                py = C.P[4 + ky[0] % 2]
                ky[0] += 1
                for fl in range(ng):
                    S.mm(py[:, :], hT[:, fl, t * 128:(t + 1) * 128], w2b[:, gi % 2, fl, half * 512:(half + 1) * 512],
                         fl == 0, fl == ng - 1)
                r = C.res[:, t, half * 512:(half + 1) * 512]
                if gi == 0:
                    S.stt(r, r, ALPHA, py[:, :], ALU.mult, ALU.add)
                else:
                    S.tt("dve", r, r, py[:, :], ALU.add)
            if last:
                ln_stats(S, C, L, t)
                if t >= 1:
                    ln_apply(S, C, L, t - 1, to_xT=(y is None), y=y, eng="mix")
                if t == NT - 1:
                    ln_apply(S, C, L, t, to_xT=(y is None), y=y, eng="mix")

        kk = [0]
        ky = [0]
        k = 0
        for gi, (f0, f1) in enumerate(groups):
            ng = f1 - f0
            last = gi == len(groups) - 1
            wbs = {}
            for fp in range(f0 // 2, f1 // 2):
                wb = k % 3
                k += 1
                wbs[fp] = wb
                if k == 1:
                    va = wview(w13[layer], fp * 256, 256)
                    vb_ = wview(w13[layer], DFF + fp * 256, 256)
                    for cc in range(0, 8, 2):
                        S.dma("pool", wt[:, wb, cc:cc + 2, 0, :], va[:, cc:cc + 2, :])
                    for cc in range(0, 8, 2):
                        S.dma("pool", wt[:, wb, cc:cc + 2, 1, :], vb_[:, cc:cc + 2, :])
                else:
                    S.dma("pool", wt[:, wb, :, 0, :], wview(w13[layer], fp * 256, 256))
                    S.dma("pool", wt[:, wb, :, 1, :], wview(w13[layer], DFF + fp * 256, 256))
                fl0 = 2 * fp - f0
                S.dma("pool", w2b[:, gi % 2, fl0:fl0 + 2, :], w2v[:, 2 * fp:2 * fp + 2, :])
                for ff in range(2):
                    for tb in range(4):
                        w13_block(wb, 2 * fp + ff, ff, tb, f0)
            for t in range(NT):
                w2_tile(gi, ng, t, last)
        S.emit()


def phase_rglru(nc, C, W, dbg=None):
    w_in, w_out = W["od_w_in"], W["od_w_out"]
    GK = 2.0 * math.sqrt(2.0 / math.pi)
    with contextlib.ExitStack() as st:
        T = mkT(nc, st)
        S = Sched(nc)
        wx = T("r_wx", [128, 8, 512], BF16)
        wg = T("r_wg", [128, 8, 512], BF16)
        wo = T("r_wo", [128, 4, 1024], BF16)
        bda = T("r_bda", [128, 4, 128], BF16)
        bdx = T("r_bdx", [128, 4, 128], BF16)
        cw = T("r_cw", [128, 4, 4], F32)
        cb = T("r_cb", [128, 4], F32)
        ba = T("r_ba", [128, 4], F32)
        bx = T("r_bx", [128, 4], F32)
        lam = T("r_lam", [128, 4], F32)
        sp = T("r_sp", [128, 6, 4], F32)
        xr = T("r_xr", [128, 3 + S_TOK], F32)
        gl = T("r_gl", [128, 2, S_TOK], BF16)
        om = T("r_om", [128, S_TOK], F32)
        xc = T("r_xc", [128, S_TOK], F32)
        xcb = T("r_xcb", [128, S_TOK], BF16)
        at = T("r_at", [128, S_TOK], F32)
        ut = T("r_ut", [128, S_TOK], F32)
        tg = T("r_tg", [128, 2, 512], F32)
        tq = T("r_tq", [128, 2, 512], F32)
        yc = T("r_yc", [128, 4, S_TOK], BF16)
        wload(S, wx, w_in, 0, 512)
        wload(S, wg, w_in, 512, 512)
        S.dma("pool", wo[:], w_out[0:512, :].rearrange("(c p) n -> p c n", p=128))
        S.dma("pool", bda[:], W["bd_a"])
        S.dma("pool", bdx[:], W["bd_x"])
        S.dma("sp", cw[:], W["cw"])
        S.dma("sp", cb[:], W["cb"])
        S.dma("sp", ba[:], W["ba"])
        S.dma("sp", bx[:], W["bx"])
        S.dma("sp", lam[:], W["lam"])
        y_, s_, s2, pl, t0 = sp[:, 0, :], sp[:, 1, :], sp[:, 2, :], sp[:, 3, :], sp[:, 4, :]
        S.act(y_, lam[:], AF.Exp, scale=-1.0)
        S.ts("dve", t0, y_, 2.0, None, ALU.add)
        S.recip(t0, t0)
        S.tt("dve", s_, y_, t0, ALU.mult)
        S.tt("dve", s2, s_, s_, ALU.mult)
        S.ts("dve", pl, s2, 1.0 / 13.0, 1.0 / 11.0, ALU.mult, ALU.add)
        for cf in (1.0 / 9.0, 1.0 / 7.0, 1.0 / 5.0, 1.0 / 3.0, 1.0):
            S.tt("dve", pl, pl, s2, ALU.mult)
            S.ts("dve", pl, pl, cf, None, ALU.add)
        S.tt("dve", pl, pl, s_, ALU.mult)
        m8 = sp[:, 5, :]
        S.ts("dve", m8, pl, -16.0, None, ALU.mult)
        S.ts("dve", t0, pl, -32.0, None, ALU.mult)
        m16 = t0
        S.memset("pool", xr[:, 0:3], 0.0)
        kc = [0]

        def st_g(ch):
            for g in range(4):
                gc = slice(g * 512, (g + 1) * 512)
                k = kc[0]
                kc[0] += 1
                pa = C.P[k % 2]
                pb = C.P[2 + k % 2]
                b = k % 2
                for c in range(8):
                    S.mm(pa[:, :], wx[:, c, ch * 128:(ch + 1) * 128], C.xT[:, c, gc], c == 0, c == 7)
                for c in range(8):
                    S.mm(pb[:, :], wg[:, c, ch * 128:(ch + 1) * 128], C.xT[:, c, gc], c == 0, c == 7)
                S.cp("act", xr[:, 3 + g * 512:3 + (g + 1) * 512], pa[:, :])
                S.cp("act", tg[:, b, :], pb[:, :])
                S.act(tq[:, b, :], pb[:, :], AF.Square)
                S.ts("dve", tq[:, b, :], tq[:, b, :], 0.044715, 1.0, ALU.mult, ALU.add)
                S.tt("dve", tq[:, b, :], tq[:, b, :], tg[:, b, :], ALU.mult)
                S.act(tq[:, b, :], tq[:, b, :], AF.Sigmoid, scale=GK)
                S.tt("dve", gl[:, ch % 2, gc], tq[:, b, :], tg[:, b, :], ALU.mult)

        def st_conv(ch):
            S.ts("dve", xc[:], xr[:, 0:S_TOK], cw[:, ch, 0:1], cb[:, ch:ch + 1], ALU.mult, ALU.add)
            for j in range(1, 4):
                S.stt(xc[:], xr[:, j:j + S_TOK], cw[:, ch, j:j + 1], xc[:], ALU.mult, ALU.add)
            S.cp("act", xcb[:], xc[:])

        def st_gates(ch):
            for g in range(4):
                gc = slice(g * 512, (g + 1) * 512)
                pa = C.P[4]
                pb = C.P[5]
                S.mm(pa[:, :], bda[:, ch, :], xcb[:, gc], True, True)
                S.mm(pb[:, :], bdx[:, ch, :], xcb[:, gc], True, True)
                S.act(at[:, gc], pa[:, :], AF.Sigmoid, bias=ba[:, ch:ch + 1], scale=1.0)
                S.act(ut[:, gc], pb[:, :], AF.Sigmoid, bias=bx[:, ch:ch + 1], scale=1.0)
                S.tt("pool", ut[:, gc], ut[:, gc], xc[:, gc], ALU.mult)
            for g in range(4):
                gc = slice(g * 512, (g + 1) * 512)
                S.act(om[:, gc], at[:, gc], AF.Exp, scale=m16[:, ch:ch + 1])
                S.act(at[:, gc], at[:, gc], AF.Exp, scale=m8[:, ch:ch + 1])
                S.ts("dve", om[:, gc], om[:, gc], -1.0, 1.0, ALU.mult, ALU.add)
            for g in range(4):
                gc = slice(g * 512, (g + 1) * 512)
                S.act(om[:, gc], om[:, gc], AF.Sqrt)
                S.tt("dve", ut[:, gc], om[:, gc], ut[:, gc], ALU.mult)

        def st_fin(ch):
            S.scan(xc[:], at[:], ut[:], 0.0)
            S.tt("pool", yc[:, ch, :], xc[:], gl[:, ch % 2, :], ALU.mult)

        st_g(0)
        st_conv(0)
        for ch in range(4):
            if ch < 3:
                st_g(ch + 1)
            st_gates(ch)
            st_fin(ch)
            if ch < 3:
                st_conv(ch + 1)
        if dbg is not None:
            for ch in range(4):
                S.dma("sp", dbg[ch * 128:(ch + 1) * 128, :], yc[:, ch, :])
        outproj_partial(S, C, lambda t, c: yc[:, c, t * 128:(t + 1) * 128], 4, wo, first=True)
        S.emit()


def phase_sb(nc, C, W, dbg=None):
    w_in, w_out = W["od_w_in"], W["od_w_out"]
    SCALE = 0.125
    with contextlib.ExitStack() as st:
        T = mkT(nc, st)
        qT = T("s_qT", [128, 4, S_TOK], BF16)
        kT = T("s_kT", [128, 4, S_TOK], BF16)
        vd = T("s_vd", [128, NT, 512], BF16)
        yd = T("s_yd", [128, NT, 512], BF16)
        with contextlib.ExitStack() as st1:
            T1 = mkT(nc, st1)
            S = Sched(nc)
            wq = T1("s_wq", [128, 8, 512], BF16)
            wk = T1("s_wk", [128, 8, 512], BF16)
            wv = T1("s_wv", [128, 8, 512], BF16)
            wload(S, wq, w_in, 1024, 512)
            wload(S, wk, w_in, 1536, 512)
            S.dma("pool", wv[:], wview(w_in, 2048, 512))
            k = 0
            for c4 in range(4):
                for g in range(4):
                    gc = slice(g * 512, (g + 1) * 512)
                    pa = C.P[k % 2]
                    pb = C.P[2 + k % 2]
                    k += 1
                    for c in range(8):
                        S.mm(pa[:, :], wq[:, c, c4 * 128:(c4 + 1) * 128], C.xT[:, c, gc], c == 0, c == 7)
                    for c in range(8):
                        S.mm(pb[:, :], wk[:, c, c4 * 128:(c4 + 1) * 128], C.xT[:, c, gc], c == 0, c == 7)
                    S.cp("act", qT[:, c4, gc], pa[:, :])
                    S.cp("dve", kT[:, c4, gc], pb[:, :])
            for t in range(NT):
                pb = C.P[4 + t % 2]
                for c in range(8):
                    S.mm(pb[:, :], C.xT[:, c, t * 128:(t + 1) * 128], wv[:, c, :], c == 0, c == 7)
                S.cp("act", vd[:, t, :], pb[:, :])
            S.emit()
        with contextlib.ExitStack() as st2:
            T2 = mkT(nc, st2)
            S = Sched(nc)
            A = T2("s_A", [128, 3072], F32)
            CF = T2("s_CF", [128, 3088], F32)
            wt = T2("s_wt", [128, 2, S_TOK], BF16)
            wTt = T2("s_wTt", [128, 2, S_TOK], BF16)
            ntot = T2("s_ntot", [128, 4, 1], F32)
            posA = [0]
            posC = [0]
            zb = [0]
            qn = [0]
            order_a = []
            order_b = []
            for k in range(8):
                order_a += [15 - k, k]
                order_b += [8 + k, 7 - k]
            work = [(h, i) for h in range(8) for i in (order_a if h % 2 == 0 else order_b)]
            prevA = [None]
            prevC = [None]

            def stage_a(it, h, i):
                c4 = h // 2
                pr = slice(64 * (h % 2), 64 * (h % 2) + 64)
                n = (i + 1) * 128
                nkb = (n + 511) // 512
                if it % 2 == 0:
                    a0, c0 = 0, 0
                else:
                    assert n <= 1024
                    a0, c0 = 2048, 2056
                S.memset("pool", CF[:, c0:c0 + 1], 0.0)
                for kb in range(nkb):
                    w = min(512, n - kb * 512)
                    k0 = kb * 512
                    pz = C.P[zb[0] % 5]
                    zb[0] += 1
                    S.mm(pz[:, 0:w], qT[pr, c4, i * 128:(i + 1) * 128], kT[pr, c4, k0:k0 + w], True, True)
                    Ac = A[:, a0 + k0:a0 + k0 + w]
                    S.act(Ac, pz[:, 0:w], AF.Exp, scale=SCALE)
                    S.act(Ac, Ac, AF.Ln, bias=1.0, scale=1.0)
                    if kb == nkb - 1:
                        dgA = A[:, a0 + i * 128:a0 + (i + 1) * 128]
                        S.aselect(dgA, dgA, [[-1, 128]], ALU.is_gt, 0.0, 0, 1)
                    S.scan(CF[:, c0 + 1 + k0:c0 + 1 + k0 + w], C.ones_f[:, 0:w], Ac, CF[:, c0 + k0:c0 + k0 + 1])
                    S.stt(Ac, pz[:, 0:w], SCALE, CF[:, c0 + k0:c0 + k0 + w], ALU.mult, ALU.add)
                nt_ = ntot[:, it % 4, :]
                S.ts("dve", nt_, CF[:, c0 + n:c0 + n + 1], -1.0, None, ALU.mult)
                return (a0, nt_)

            def stage_b(it, h, i, a0, nt_):
                n = (i + 1) * 128
                wb = it % 2
                S.act(wt[:, wb, 0:n], A[:, a0:a0 + n], AF.Exp, bias=nt_, scale=1.0)
                dg = slice(i * 128, (i + 1) * 128)
                S.aselect(wt[:, wb, dg], wt[:, wb, dg], [[-1, 128]], ALU.is_gt, 0.0, 0, 1)
                src = wt[:, wb, 0:n]
                dst = wTt[:, wb, 0:n].rearrange("p (j t) -> p j t", t=128)
                S._add("sp", lambda e: e.dma_start_transpose(out=dst, in_=src), [src], [wTt[:, wb, 0:n]],
                       is_dma=True, tag=("wTt", wb))

            def stage_c(it, h, i):
                wb = it % 2
                po = C.P[5]
                for j in range(i + 1):
                    S.mm(po[:, 0:64], wTt[:, wb, j * 128:(j + 1) * 128], vd[:, j, h * 64:(h + 1) * 64], j == 0, j == i)
                S.cp("act", yd[:, i, h * 64:(h + 1) * 64], po[:, 0:64])

            pend_b = None
            pend_c = None
            for it, (h, i) in enumerate(work):
                ra = stage_a(it, h, i)
                if pend_b is not None:
                    stage_b(*pend_b)
                if pend_c is not None:
                    stage_c(*pend_c)
                pend_c = pend_b[:3] if pend_b is not None else None
                pend_b = (it, h, i) + ra
            stage_b(*pend_b)
            if pend_c is not None:
                stage_c(*pend_c)
            stage_c(*pend_b[:3])
            S.emit()
        with contextlib.ExitStack() as st3:
            T3 = mkT(nc, st3)
            S = Sched(nc)
            wo = T3("s_wo", [128, 4, 1024], BF16)
            ytile = T3("s_ytile", [128, 2, 4, 128], BF16)
            L = alloc_ln(nc, st3, C, S, W["ln_g"], W["ln_b"], 1, 0) if dbg is None else None
            wov = w_out[512:1024, :].rearrange("(c p) n -> p c n", p=128)
            for c in range(4):
                S.dma("pool", wo[:, c, :], wov[:, c, :])
            post_outproj_ln(S, C, L, yd, wo, ytile, dbg)
            S.emit()


DEBUG_MAP = {"hgrn": [("l0_ya", "dbg")], "fox": [("l0_yb", "dbg")], "l0mix": [("l0_x1", "y")],
             "l0": [("l0_x2", "y")], "full": [("l1_x4", "y")],
             "rglru": [("l1_ycT", "dbg")], "sb": [("l1_yd", "dbg")]}


def build(stage="full"):
    nc = bass.Bass("TRN2", target_bir_lowering=False)
    dt_in = lambda name, shape: nc.dram_tensor(name, list(shape), F32, kind="ExternalInput").ap()
    x = dt_in("x", [S_TOK, D])
    W = {}
    W["ev_w_in"] = dt_in("ev_w_in", [D, 3592])
    W["lb3"] = dt_in("lb3", [128, 3, 4])
    W["normg"] = dt_in("normg", [1, 256])
    W["fox_bf"] = dt_in("fox_bf", [8, 1])
    W["ev_w_out"] = dt_in("ev_w_out", [D, D])
    W["ln_g"] = dt_in("ln_g", [2, 2, D])
    W["ln_b"] = dt_in("ln_b", [2, 2, D])
    W["ffn_w13"] = dt_in("ffn_w13", [2, D, 2 * DFF])
    W["ffn_w2"] = dt_in("ffn_w2", [2, DFF, D])
    W["od_w_in"] = dt_in("od_w_in", [D, 2560])
    W["od_w_out"] = dt_in("od_w_out", [D, D])
    W["bd_a"] = dt_in("bd_a", [128, 4, 128])
    W["bd_x"] = dt_in("bd_x", [128, 4, 128])
    W["cw"] = dt_in("cw", [128, 4, 4])
    W["cb"] = dt_in("cb", [128, 4])
    W["ba"] = dt_in("ba", [128, 4])
    W["bx"] = dt_in("bx", [128, 4])
    W["lam"] = dt_in("lam", [128, 4])
    dbg = None
    if stage in ("hgrn", "fox", "sb"):
        dbg = nc.dram_tensor("dbg", [S_TOK, 512], BF16, kind="ExternalOutput").ap()
    if stage == "rglru":
        dbg = nc.dram_tensor("dbg", [512, S_TOK], BF16, kind="ExternalOutput").ap()
    y = nc.dram_tensor("y", [S_TOK, D], F32, kind="ExternalOutput").ap()

    with contextlib.ExitStack() as st:
        T = mkT(nc, st)
        C = Ctx()
        C.res = T("res", [128, NT, D], F32)
        C.xT = T("xT", [128, 8, S_TOK], BF16)
        C.ident_f = T("ident_f", [128, 128], F32)
        C.ident_b = T("ident_b", [128, 128], BF16)
        C.ones_f = T("ones_f", [128, 512], F32)
        C.P = [st.enter_context(nc.psum_tensor("P%d" % i, [128, 512], F32)) for i in range(6)]
        C.Q = [st.enter_context(nc.psum_tensor("Q%d" % i, [128, 1024], BF16)) for i in range(2)]
        C.P = C.P + [C.P[4], C.P[5]]

        def dump_res():
            S = Sched(nc)
            yv = y.rearrange("(t p) d -> p t d", p=128)
            for i in range(4):
                S.dma("sp", yv[:, 4 * i:4 * i + 4, :], C.res[:, 4 * i:4 * i + 4, :])
            S.emit()

        merged = stage in ("full", "l0", "l0mix", "hgrn")
        if not merged:
            phase_load(nc, C, x)
        if stage.startswith("p:"):
            plist = stage[2:].split(",")
            wrote = False
            for p in plist:
                if p == "hgrn0":
                    phase_hgrn_pair(nc, C, W, 0)
                elif p == "hgrn1":
                    phase_hgrn_pair(nc, C, W, 1)
                elif p == "fox":
                    phase_fox(nc, C, W)
                elif p == "ln00":
                    pass
                elif p == "ffn0":
                    phase_ffn(nc, C, W, 0)
                elif p == "rglru":
                    phase_rglru(nc, C, W)
                elif p == "sb":
                    phase_sb(nc, C, W)
                elif p == "ln10":
                    pass
                elif p == "ffn1":
                    phase_ffn(nc, C, W, 1, y)
                    wrote = True
            if not wrote:
                dump_res()
            return nc
        if stage in ("rglru", "sb"):
            if stage == "rglru":
                phase_rglru(nc, C, W, dbg)
            else:
                phase_sb(nc, C, W, dbg)
            dump_res()
            return nc
        if stage != "fox":
            phase_hgrn_pair(nc, C, W, 0, dbg if stage == "hgrn" else None, x=x)
            phase_hgrn_pair(nc, C, W, 1, dbg if stage == "hgrn" else None)
        if stage == "hgrn":
            dump_res()
            return nc
        phase_fox(nc, C, W, dbg if stage == "fox" else None)
        if stage == "fox":
            dump_res()
            return nc
        if stage == "l0mix":
            dump_res()
            return nc
        phase_ffn(nc, C, W, 0, y if stage == "l0" else None)
        if stage == "l0":
            return nc
        phase_rglru(nc, C, W)
        phase_sb(nc, C, W)
        phase_ffn(nc, C, W, 1, y)
    return nc


def _blockdiag(w):
    w = np.asarray(w, dtype=np.float32)
    out = np.zeros((128, 4, 128), dtype=np.float32)
    for n in range(8):
        ch, o = n // 2, 64 * (n % 2)
        out[o:o + 64, ch, o:o + 64] = w[n]
    return out


def host_inputs(inputs):
    f = lambda a: np.ascontiguousarray(np.asarray(a, dtype=np.float32))
    shared = {
        "ev_w_in": f(inputs["ev_w_in"][0]),
        "lb3": f(np.asarray(inputs["hgrn_lb"]).reshape(3, 4, 128).transpose(2, 0, 1)),
        "normg": f(np.tile(np.asarray(inputs["ev_hgrn_norm_g"][0]), 2)[None, :]),
        "ev_w_out": f(inputs["ev_w_out"][0]),
        "fox_bf": f(np.asarray(inputs["ev_fox_bf"][0]).reshape(8, 1)),
        "ffn_w13": f(inputs["ffn_w13"]),
        "ffn_w2": f(inputs["ffn_w2"]),
        "od_w_in": f(inputs["od_w_in"][0]),
        "od_w_out": f(inputs["od_w_out"][0]),
        "bd_a": f(_blockdiag(inputs["od_gate_a_w"][0])),
        "bd_x": f(_blockdiag(inputs["od_gate_x_w"][0])),
        "cw": f(np.asarray(inputs["od_conv_w"][0]).reshape(4, 4, 128).transpose(2, 1, 0)),
        "cb": f(np.asarray(inputs["od_conv_b"][0]).reshape(4, 128).T),
        "ba": f(np.asarray(inputs["od_gate_a_b"][0]).reshape(4, 128).T),
        "bx": f(np.asarray(inputs["od_gate_x_b"][0]).reshape(4, 128).T),
        "lam": f(np.asarray(inputs["od_lru_lambda"][0]).reshape(4, 128).T),
        "ln_g": f(inputs["ln_g"]),
        "ln_b": f(inputs["ln_b"]),
    }
    return shared


def kernel(**inputs):
    shared = host_inputs(inputs)
    x = np.asarray(inputs["x"], dtype=np.float32)
    nc = build("full")
    in_maps = [dict(shared, x=np.ascontiguousarray(x[i])) for i in range(8)]
    res = run_bass_kernel_spmd(nc, in_maps, core_ids=list(range(8)))
    return np.stack([r["y"] for r in res.results], axis=0)
```

```python
import contextlib
import math
import numpy as np
import concourse.bass as bass
import concourse.mybir as mybir
from concourse.bass_utils import run_bass_kernel_spmd

F32 = mybir.dt.float32
BF16 = mybir.dt.bfloat16
AF = mybir.ActivationFunctionType
ALU = mybir.AluOpType
AX = mybir.AxisListType

ENGS = ("pe", "act", "dve", "pool", "sp")
INLINE_WAIT = True
S_TOK = 2048
D = 1024
NT = 16
ALPHA = 4.0 ** 0.25
EPS = 1e-5
DFF = 2816
NF = 22


def _geom(ap):
    t = ap.tensor
    dims = ap.ap
    off = int(ap.offset)
    space = str(ap.space)
    if "PSUM" in space:
        return (t.name, 0, 128, 0, 1 << 30, True)
    if "SB" in space:
        pstep, pcnt = dims[0]
        if pstep == 0:
            p0, f0 = 0, off
        else:
            p0, f0 = off // pstep, off % pstep
        rest = dims[1:]
        p1 = p0 + pcnt
    else:
        p0, p1 = 0, 1
        f0 = off
        rest = dims
    ext = 1
    lo = 0
    for st, cnt in rest:
        if cnt > 1:
            if st >= 0:
                ext += st * (cnt - 1)
            else:
                lo += st * (cnt - 1)
    return (t.name, p0, p1, f0 + lo, f0 + ext, False)


class Op:
    __slots__ = ("eng", "fn", "waits", "is_dma", "sem", "val", "needed", "waited")


class Sched:
    def __init__(self, nc):
        self.nc = nc
        self.ops = {e: [] for e in ENGS}
        self.acc = {}
        self.dma_cnt = {}

    def _deps(self, op, reads, writes):
        deps = []
        for ap, is_w in [(a, False) for a in reads] + [(a, True) for a in writes]:
            name, p0, p1, f0, f1, excl = _geom(ap)
            if excl:
                is_w = True
            lst = self.acc.get(name, [])
            new = []
            for ent in lst:
                q0, q1, g0, g1, o, w = ent
                overlap = not (p1 <= q0 or q1 <= p0 or f1 <= g0 or g1 <= f0)
                if overlap and (is_w or w) and o is not op:
                    deps.append((o, w and not excl, is_w and not excl))
                if is_w and overlap and p0 <= q0 and q1 <= p1 and f0 <= g0 and g1 <= f1:
                    continue
                new.append(ent)
            new.append([p0, p1, f0, f1, op, is_w])
            self.acc[name] = new
        return deps

    def _add(self, eng, fn, reads, writes, is_dma=False, tag=None):
        op = Op()
        op.eng = eng
        op.fn = fn
        op.is_dma = is_dma
        op.needed = is_dma
        op.waited = False
        op.waits = []
        if is_dma:
            c = self.dma_cnt.get(tag, 0) + 16
            self.dma_cnt[tag] = c
            op.sem = ("dma", tag)
            op.val = c
        else:
            op.sem = ("eng", eng)
            op.val = None
        seen = set()
        for o, o_w, me_w in self._deps(op, reads, writes):
            if id(o) in seen:
                continue
            if not o.is_dma and not is_dma and o.eng == eng:
                if eng == "pe":
                    continue
                if eng != "pool" and not (o_w and not me_w):
                    continue
            seen.add(id(o))
            o.needed = True
            o.waited = True
            op.waits.append(o)
        self.ops[eng].append(op)
        return op

    def op(self, eng, fn, reads=(), writes=()):
        return self._add(eng, fn, list(reads), list(writes))

    def dma(self, eng, out, in_, tag=None, **kw):
        if tag is None:
            sb = out if "DRAM" not in str(out.space).upper() else in_
            tag = _geom(sb)[:4]
        return self._add(eng, lambda e: e.dma_start(out=out, in_=in_, **kw), [in_], [out],
                         is_dma=True, tag=tag)

    def emit(self):
        nc = self.nc
        for e in ENGS:
            c = 0
            for o in self.ops[e]:
                if not o.is_dma and o.needed:
                    c += 1
                    o.val = c
        sem_keys = [("eng", e) for e in ENGS if any((not o.is_dma) and o.needed for o in self.ops[e])]
        sem_keys += [("dma", t) for t in self.dma_cnt]
        assert len(sem_keys) < 200, len(sem_keys)
        sems = {}
        for i, k in enumerate(sem_keys):
            _UID[0] += 1
            sems[k] = nc.alloc_semaphore(name="s%d_%d" % (i, _UID[0]))
        with contextlib.ExitStack() as st:
            block = st.enter_context(nc.Block())

            def run(e, eobj):
                seen = {}
                for o in self.ops[e]:
                    need = {}
                    for d in o.waits:
                        if d.val > need.get(d.sem, 0):
                            need[d.sem] = d.val
                    pend_w = []
                    for k_, v_ in need.items():
                        if seen.get(k_, 0) >= v_:
                            continue
                        seen[k_] = v_
                        pend_w.append((k_, v_))
                    inline = INLINE_WAIT and (not o.is_dma) and len(pend_w) >= 1
                    for k_, v_ in (pend_w[:-1] if inline else pend_w):
                        eobj.wait_ge(sems[k_], v_)
                    ins = o.fn(eobj)
                    if inline:
                        ins._wait_ge(sems[pend_w[-1][0]], pend_w[-1][1])
                    if o.is_dma:
                        ins.then_inc(sems[o.sem], 16)
                    elif o.needed:
                        ins.then_inc(sems[o.sem], 1)
                last = {}
                for o in self.ops[e]:
                    if o.is_dma:
                        last[o.sem] = o.val
                for k, v in last.items():
                    if seen.get(k, 0) < v:
                        eobj.wait_ge(sems[k], v)

            if self.ops["pe"]:
                block.tensor(lambda t: run("pe", t))
            if self.ops["act"]:
                block.scalar(lambda t: run("act", t))
            if self.ops["dve"]:
                block.vector(lambda t: run("dve", t))
            if self.ops["pool"]:
                block.gpsimd(lambda t: run("pool", t))
            if self.ops["sp"]:
                block.sync(lambda t: run("sp", t))
        nc.all_engine_barrier()
        nc.clear_and_free_semaphores(list(sems.values()))
        nc.all_engine_barrier()
        return {e: len(self.ops[e]) for e in ENGS}

    def mm(self, out, lhsT, rhs, start, stop):
        return self.op("pe", lambda e: e.matmul(out, lhsT=lhsT, rhs=rhs, start=start, stop=stop),
                       reads=[lhsT, rhs], writes=[out])

    def tr(self, out, in_, ident):
        return self.op("pe", lambda e: e.transpose(out=out, in_=in_, identity=ident),
                       reads=[in_, ident], writes=[out])

    def act(self, out, in_, func, bias=None, scale=None, accum_out=None):
        kw = {}
        rd = [in_]
        wr = [out]
        if bias is not None:
            kw["bias"] = bias
            if not isinstance(bias, (int, float)):
                rd.append(bias)
        if scale is not None:
            kw["scale"] = scale
            if not isinstance(scale, (int, float)):
                rd.append(scale)
        if accum_out is not None:
            kw["accum_out"] = accum_out
            wr.append(accum_out)
        return self.op("act", lambda e: e.activation(out=out, in_=in_, func=func, **kw), reads=rd, writes=wr)

    def tt(self, eng, out, a, b, op):
        return self.op(eng, lambda e: e.tensor_tensor(out=out, in0=a, in1=b, op=op), reads=[a, b], writes=[out])

    def ts(self, eng, out, a, s1, s2, op0, op1=None):
        rd = [a] + [s for s in (s1, s2) if s is not None and not isinstance(s, (int, float))]
        if op1 is None:
            return self.op(eng, lambda e: e.tensor_scalar(out=out, in0=a, scalar1=s1, scalar2=None, op0=op0),
                           reads=rd, writes=[out])
        return self.op(eng, lambda e: e.tensor_scalar(out=out, in0=a, scalar1=s1, scalar2=s2, op0=op0, op1=op1),
                       reads=rd, writes=[out])

    def stt(self, out, in0, scalar, in1, op0, op1, eng="dve"):
        rd = [in0, in1] + ([] if isinstance(scalar, (int, float)) else [scalar])
        return self.op(eng, lambda e: e.scalar_tensor_tensor(out=out, in0=in0, scalar=scalar, in1=in1, op0=op0, op1=op1),
                       reads=rd, writes=[out])

    def cp(self, eng, out, in_):
        if eng == "act":
            return self.op("act", lambda e: e.copy(out=out, in_=in_), reads=[in_], writes=[out])
        return self.op(eng, lambda e: e.tensor_copy(out=out, in_=in_), reads=[in_], writes=[out])

    def scan(self, out, d0, d1, initial, op0=ALU.mult, op1=ALU.add):
        rd = [d0, d1] + ([] if isinstance(initial, (int, float)) else [initial])
        return self.op("dve", lambda e: e.tensor_tensor_scan(out=out, data0=d0, data1=d1, initial=initial, op0=op0, op1=op1),
                       reads=rd, writes=[out])

    def memset(self, eng, ap, v):
        return self.op(eng, lambda e: e.memset(ap, v), writes=[ap])

    def recip(self, out, in_):
        return self.op("dve", lambda e: e.reciprocal(out=out, in_=in_), reads=[in_], writes=[out])

    def aselect(self, out, in_, pattern, cmp, fill, base, cm):
        return self.op("pool", lambda e: e.affine_select(out=out, in_=in_, pattern=pattern, compare_op=cmp,
                                                         fill=fill, base=base, channel_multiplier=cm),
                       reads=[in_], writes=[out])


class Ctx:
    pass


_UID = [0]


def mkT(nc, st):
    def T(name, shape, dt):
        _UID[0] += 1
        return st.enter_context(nc.sbuf_tensor("%s_%d" % (name, _UID[0]), shape, dt))
    return T


def wload(S, dst, w, c0, n, parts=4):
    v = wview(w, c0, n)
    step = 8 // parts
    for i in range(parts):
        S.dma("pool", dst[:, i * step:(i + 1) * step, :], v[:, i * step:(i + 1) * step, :])


def wview(w, c0, n):
    return w[:, c0:c0 + n].rearrange("(c p) n -> p c n", p=128)


def phase_load(nc, C, x, S=None):
    own = S is None
    if own:
        S = Sched(nc)
    xv = x.rearrange("(t p) d -> p t d", p=128)
    for i in range(4):
        S.dma("sp", C.res[:, 4 * i:4 * i + 4, :], xv[:, 4 * i:4 * i + 4, :])
    S.memset("pool", C.ident_f[:], 1.0)
    S.aselect(C.ident_f[:], C.ident_f[:], [[-1, 128]], ALU.is_equal, 0.0, 0, 1)
    S.cp("pool", C.ident_b[:], C.ident_f[:])
    S.memset("pool", C.ones_f[:], 1.0)
    for t in range(NT):
        transpose_to_xT(S, C, t)
    if own:
        S.emit()


def transpose_to_xT(S, C, t):
    for half in range(2):
        pt = C.P[6 + half]
        for j in range(4):
            c = half * 4 + j
            S.tr(pt[:, j * 128:(j + 1) * 128], C.res[:, t, c * 128:(c + 1) * 128], C.ident_f[:])
        S.cp("act", C.xT[:, half * 4:half * 4 + 4, t * 128:(t + 1) * 128],
             pt[:].rearrange("p (c n) -> p c n", c=4))


def ln_stats(S, C, L, t):
    st = L.stats[:, t % 2]
    S.op("dve", lambda e: e.bn_stats(out=st[:, 0, :], in_=C.res[:, t, 0:512]), reads=[C.res[:, t, 0:512]], writes=[st[:, 0, :]])
    S.op("dve", lambda e: e.bn_stats(out=st[:, 1, :], in_=C.res[:, t, 512:1024]), reads=[C.res[:, t, 512:1024]], writes=[st[:, 1, :]])
    mv = L.mv[:, t % 2, :]
    S.op("dve", lambda e: e.bn_aggr(out=mv, in_=st.rearrange("p a b -> p (a b)")), reads=[st], writes=[mv])
    sd = L.sd[:, t % 2, :]
    S.act(sd, mv[:, 1:2], AF.Sqrt, bias=L.eps[:, 0:1], scale=1.0)


def ln_apply(S, C, L, t, to_xT=True, y=None, eng="pool"):
    r = C.res[:, t, :]
    mv = L.mv[:, t % 2, :]
    sd = L.sd[:, t % 2, :]
    nb = L.nb[:, t % 2, :]
    S.recip(sd, sd)
    S.stt(nb, mv[:, 0:1], -1.0, sd, ALU.mult, ALU.mult)
    tmp = L.tmp[:, t % 2, :]
    S.act(tmp, r, AF.Identity, bias=nb, scale=sd)
    S.tt("dve", tmp, tmp, L.g[:], ALU.mult)
    S.tt("pool", r, tmp, L.b[:], ALU.add)
    if y is not None:
        S.dma("sp", y[t * 128:(t + 1) * 128, :], C.res[:, t, :])
    elif to_xT:
        xb = L.xb[:, t % 2, :]
        S.tt("dve", xb, tmp, L.b[:], ALU.add)
        dst = C.xT[:, :, t * 128:(t + 1) * 128]
        S._add("sp", lambda e: e.dma_start_transpose(out=dst, in_=xb), [xb], [dst], is_dma=True, tag=("xTtr", t % 2))


def ln_tile(S, C, L, t, to_xT=True, y=None, eng="pool"):
    ln_stats(S, C, L, t)
    ln_apply(S, C, L, t, to_xT, y, eng)


def alloc_ln(nc, st, C, S, ln_g, ln_b, layer, which):
    L = Ctx()
    T = mkT(nc, st)
    L.g = T("ln_g%d%d" % (layer, which), [128, 1024], F32)
    L.b = T("ln_b%d%d" % (layer, which), [128, 1024], F32)
    L.stats = T("ln_st%d%d" % (layer, which), [128, 2, 2, 6], F32)
    L.mv = T("ln_mv%d%d" % (layer, which), [128, 2, 2], F32)
    L.sd = T("ln_sd%d%d" % (layer, which), [128, 2, 1], F32)
    L.nb = T("ln_nb%d%d" % (layer, which), [128, 2, 1], F32)
    L.eps = T("ln_eps%d%d" % (layer, which), [128, 1], F32)
    L.tmp = T("ln_tmp%d%d" % (layer, which), [128, 2, 1024], F32)
    L.xb = T("ln_xb%d%d" % (layer, which), [128, 2, 1024], BF16)
    S.dma("sp", L.g[:], ln_g[layer, which:which + 1, :].partition_broadcast(128))
    S.dma("sp", L.b[:], ln_b[layer, which:which + 1, :].partition_broadcast(128))
    S.memset("pool", L.eps[:], EPS)
    return L


def outproj_partial(S, C, yT_fn, nchunk, wo, first):
    k = 0
    for t in range(NT):
        for half in range(2):
            pb = C.P[4 + (k % 2)]
            k += 1
            for c in range(nchunk):
                S.mm(pb[:, :], yT_fn(t, c), wo[:, c, half * 512:(half + 1) * 512], c == 0, c == nchunk - 1)
            r = C.res[:, t, half * 512:(half + 1) * 512]
            if first:
                S.stt(r, r, ALPHA, pb[:, :], ALU.mult, ALU.add)
            else:
                S.tt("dve", r, r, pb[:, :], ALU.add)


def post_outproj_ln(S, C, L, ytm, wo, ytile, dbg=None):
    def stage1(t):
        q = C.Q[t % 2]
        for c in range(4):
            S.tr(q[:, c * 128:(c + 1) * 128], ytm[:, t, c * 128:(c + 1) * 128], C.ident_b[:])
        S.cp("act", ytile[:, t % 2, :, :], q[:, 0:512].rearrange("p (c n) -> p c n", c=4))
        if dbg is not None:
            S.dma("sp", dbg[t * 128:(t + 1) * 128, :], ytm[:, t, :])
        for half in range(2):
            pb = C.P[2 * (t % 2) + half]
            for c in range(4):
                S.mm(pb[:, :], ytile[:, t % 2, c, :], wo[:, c, half * 512:(half + 1) * 512], c == 0, c == 3)
            r = C.res[:, t, half * 512:(half + 1) * 512]
            S.tt("dve", r, r, pb[:, :], ALU.add)

    for t in range(NT + 2):
        if t < NT:
            stage1(t)
        if L is not None and 1 <= t <= NT:
            ln_stats(S, C, L, t - 1)
        if L is not None and t >= 2:
            ln_apply(S, C, L, t - 2)


def phase_hgrn_pair(nc, C, W, hp, dbg=None, x=None):
    w_in, w_out = W["ev_w_in"], W["ev_w_out"]
    with contextlib.ExitStack() as st:
        T = mkT(nc, st)
        S = Sched(nc)
        if x is not None:
            phase_load(nc, C, x, S)
        wq = T("h_wq", [128, 8, 256], BF16)
        wf = T("h_wf", [128, 8, 256], BF16)
        wv = T("h_wv", [128, 8, 256], BF16)
        wg = T("h_wg", [128, 8, 256], BF16)
        wo = T("h_wo", [128, 2, 1024], BF16)
        qt = T("h_qt", [128, 2, S_TOK], BF16)
        kt = T("h_kt", [128, 2, S_TOK], BF16)
        ktm = T("h_ktm", [128, 2, NT, 128], BF16)
        vtm = T("h_vtm", [128, NT, 256], BF16)
        gate = T("h_gate", [128, NT, 256], BF16)
        yTp = T("h_yT", [128, 2, S_TOK], BF16)
        ebl = T("h_ebl", [128, 2, 32], F32)
        lb3 = T("h_lb3", [128, 3, 4], F32)
        lbs = T("h_lbs", [128, 4, 4], F32)
        ngb = T("h_ngb", [128, 256], F32)
        tsig = T("h_sig", [128, 3, 512], F32)
        tlf = T("h_lf", [128, 3, 512], F32)
        tkk = T("h_kk", [128, 3, 512], F32)
        te = T("h_e", [128, 3, 512], F32)
        te2 = T("h_e2", [128, 3, 512], F32)
        tsg = T("h_sg", [128, 2, 256], F32)
        stf = T("h_stf", [128, 2, 128], F32)
        stb = T("h_stb", [128, 2, 128], BF16)
        stmp = T("h_stmp", [128, 2, 128], F32)
        scm = T("h_scm", [128, 2, 128], BF16)
        ss = T("h_ss", [128, 2, 2], F32)
        rstd = T("h_rstd", [128, 2, 2], F32)
        junk = T("h_junk", [128, 128], F32)
        yat = T("h_yat", [128, 2, 256], BF16)
        epsb = T("h_eps", [128, 1], F32)
        C.hmask = T("hmask", [128, 2, 128], F32)
        C.rmask = T("rmask", [128, 512], F32)
        S.memset("pool", C.hmask[:], 1.0)
        for hh_ in range(2):
            v = C.hmask[:, hh_, :]
            S.aselect(v, v, [[1, 128]], ALU.is_ge, 0.0, 0, -1)
        S.memset("pool", C.rmask[:], 1.0)
        S.memset("pool", C.rmask[:].rearrange("p (c k) -> p c k", k=128)[:, :, 0:1], 0.0)

        c0 = hp * 256
        wload(S, wq, w_in, 0 + c0, 256)
        wload(S, wf, w_in, 512 + c0, 256)
        S.dma("pool", wv[:], wview(w_in, 1024 + c0, 256))
        S.dma("pool", wg[:], wview(w_in, 1536 + c0, 256))
        S.dma("pool", wo[:], w_out[c0:c0 + 256, :].rearrange("(c p) n -> p c n", p=128))
        S.dma("sp", lb3[:], W["lb3"])
        S.dma("sp", ngb[:], W["normg"].partition_broadcast(128))
        S.memset("pool", epsb[:], EPS)
        S.act(lb3[:], lb3[:], AF.Exp)
        S.tt("dve", lbs[:, 3, :], lb3[:, 0, :], lb3[:, 1, :], ALU.add)
        S.tt("dve", lbs[:, 3, :], lbs[:, 3, :], lb3[:, 2, :], ALU.add)
        S.recip(lbs[:, 3, :], lbs[:, 3, :])
        S.tt("dve", lbs[:, 0, :], lb3[:, 0, :], lbs[:, 3, :], ALU.mult)
        S.ts("dve", lbs[:, 1, :], lbs[:, 0, :], -1.0, 1.0, ALU.mult, ALU.add)
        S.ts("dve", lbs[:, 2, :], lbs[:, 1, :], -1.0, None, ALU.mult)

        def h1_p(k, hh, g):
            h = 2 * hp + hh
            gc = slice(g * 512, (g + 1) * 512)
            pq = C.P[0 + (k % 3)]
            pf = C.P[3 + (k % 3)]
            b = k % 3
            for c in range(8):
                S.mm(pq[:, :], wq[:, c, hh * 128:(hh + 1) * 128], C.xT[:, c, gc], c == 0, c == 7)
            for c in range(8):
                S.mm(pf[:, :], wf[:, c, hh * 128:(hh + 1) * 128], C.xT[:, c, gc], c == 0, c == 7)
            S.act(tsig[:, b, :], pf[:, :], AF.Sigmoid)
            S.act(tlf[:, b, :], tsig[:, b, :], AF.Ln, bias=lbs[:, 0, h:h + 1], scale=lbs[:, 1, h:h + 1])
            S.ts("pool", tkk[:, b, :], tsig[:, b, :], lbs[:, 2, h:h + 1], lbs[:, 1, h:h + 1], ALU.mult, ALU.add)
            S.scan(tlf[:, b, :], C.rmask[:], tlf[:, b, :], 0.0)

        def h1_e(k, hh, g):
            gc = slice(g * 512, (g + 1) * 512)
            pq = C.P[0 + (k % 3)]
            b = k % 3
            S.act(te[:, b, :], tlf[:, b, :], AF.Exp)
            S.tt("dve", qt[:, hh, gc], pq[:, :], te[:, b, :], ALU.mult)
            S.cp("pool", ebl[:, hh, g * 4:(g + 1) * 4], te[:, b, :].rearrange("p (c k) -> p c k", k=128)[:, :, 127])
            S.act(te2[:, b, :], tlf[:, b, :], AF.Exp, scale=-1.0)
            S.tt("pool", kt[:, hh, gc], tkk[:, b, :], te2[:, b, :], ALU.mult)

        items = [(hh, g) for hh in range(2) for g in range(4)]
        for k, (hh, g) in enumerate(items):
            h1_p(k, hh, g)
            if k >= 2:
                h1_e(k - 2, *items[k - 2])
        h1_e(len(items) - 2, *items[-2])
        h1_e(len(items) - 1, *items[-1])
        for hh in range(2):
            for tb in range(2):
                q = C.Q[tb % 2]
                for j in range(8):
                    t = tb * 8 + j
                    S.tr(q[:, j * 128:(j + 1) * 128], kt[:, hh, t * 128:(t + 1) * 128], C.ident_b[:])
                S.cp("act", ktm[:, hh, tb * 8:(tb + 1) * 8, :], q[:].rearrange("p (t n) -> p t n", t=8))
        for t in range(NT):
            pb = C.P[t % 2]
            tc_ = slice(t * 128, (t + 1) * 128)
            for c in range(8):
                S.mm(pb[:, 0:256], C.xT[:, c, tc_], wv[:, c, :], c == 0, c == 7)
            for c in range(8):
                S.mm(pb[:, 256:512], C.xT[:, c, tc_], wg[:, c, :], c == 0, c == 7)
            S.cp("act", vtm[:, t, :], pb[:, 0:256])
            S.act(tsg[:, t % 2, :], pb[:, 256:512], AF.Silu)
            S.tt("pool", gate[:, t, :], tsg[:, t % 2, :], ngb[:], ALU.mult)

        S.memset("pool", stf[:], 0.0)
        S.memset("pool", stb[:], 0.0)

        def h2_r(t):
            tc_ = slice(t * 128, (t + 1) * 128)
            psc = C.P[0]
            ob = 1 if t % 2 == 0 else 4
            for hh in range(2):
                S.mm(psc[:, hh * 128:(hh + 1) * 128], kt[:, hh, tc_], qt[:, hh, tc_], True, True)
            S.tt("dve", scm[:].rearrange("p a b -> p (a b)"), psc[:, 0:256],
                 C.hmask[:].rearrange("p a b -> p (a b)"), ALU.mult)
            for hh in range(2):
                po = C.P[ob + hh]
                S.mm(po[:, 0:128], scm[:, hh, :], vtm[:, t, hh * 128:(hh + 1) * 128], True, False)
                S.mm(po[:, 0:128], qt[:, hh, tc_], stb[:, hh, :], False, True)
            pd = C.P[3]
            ecb = ebl[:, :, t:t + 1].to_broadcast([128, 2, 128])
            for hh in range(2):
                S.mm(pd[:, hh * 128:(hh + 1) * 128], ktm[:, hh, t, :], vtm[:, t, hh * 128:(hh + 1) * 128], True, True)
            S.op("dve", lambda e: e.tensor_tensor(out=stmp[:], in0=pd[:, 0:256].rearrange("p (a b) -> p a b", a=2),
                                                 in1=stf[:], op=ALU.add),
                 reads=[pd[:, 0:256], stf[:]], writes=[stmp[:]])
            S.op("dve", lambda e, ecb=ecb: e.tensor_tensor(out=stb[:], in0=stmp[:], in1=ecb, op=ALU.mult),
                 reads=[stmp[:], ebl[:, :, t:t + 1]], writes=[stb[:]])
            S.op("dve", lambda e, ecb=ecb: e.tensor_tensor(out=stf[:], in0=stmp[:], in1=ecb, op=ALU.mult),
                 reads=[stmp[:], ebl[:, :, t:t + 1]], writes=[stf[:]])

        def h2_o(t):
            tc_ = slice(t * 128, (t + 1) * 128)
            ob = 1 if t % 2 == 0 else 4
            for hh in range(2):
                po = C.P[ob + hh]
                S.act(junk[:], po[:, 0:128], AF.Square, accum_out=ss[:, t % 2, hh:hh + 1])
            S.act(rstd[:, t % 2, :], ss[:, t % 2, :], AF.Ln, bias=epsb[:, 0:1], scale=1.0 / 128.0)
            S.act(rstd[:, t % 2, :], rstd[:, t % 2, :], AF.Exp, scale=-0.5)
            for hh in range(2):
                po = C.P[ob + hh]
                S.stt(yat[:, t % 2, hh * 128:(hh + 1) * 128], po[:, 0:128], rstd[:, t % 2, hh:hh + 1],
                      gate[:, t, hh * 128:(hh + 1) * 128], ALU.mult, ALU.mult)
            q = C.Q[t % 2]
            for hh in range(2):
                S.tr(q[:, hh * 128:(hh + 1) * 128], yat[:, t % 2, hh * 128:(hh + 1) * 128], C.ident_b[:])
            S.cp("act", yTp[:, :, tc_], q[:, 0:256].rearrange("p (a n) -> p a n", a=2))
            if dbg is not None:
                S.dma("sp", dbg[t * 128:(t + 1) * 128, c0:c0 + 256], yat[:, t % 2, :])

        for t in range(NT):
            h2_r(t)
            if t >= 1:
                h2_o(t - 1)
        h2_o(NT - 1)
        outproj_partial(S, C, lambda t, c: yTp[:, c, t * 128:(t + 1) * 128], 2, wo, first=(hp == 0))
        S.emit()


def phase_fox(nc, C, W, dbg=None):
    w_in, w_out = W["ev_w_in"], W["ev_w_out"]
    SCALE = 0.125
    with contextlib.ExitStack() as st:
        T = mkT(nc, st)
        vb = T("f_vb", [128, NT * 8, 65], BF16)
        ybtm = T("f_ybtm", [128, NT, 512], BF16)
        wo = T("f_wo", [128, 4, 1024], BF16)
        crow = T("f_crow", [8, S_TOK], BF16)
        cnegT = T("f_cnegT", [128, NT, 8], F32)
        negbf = T("f_negbf", [8, 1], F32)
        st1 = st.enter_context(contextlib.ExitStack())
        T1 = mkT(nc, st1)
        S = Sched(nc)
        wv = T1("f_wv", [128, 8, 512], BF16)
        wfb = T1("f_wfb", [128, 8, 8], BF16)
        cneg = T1("f_cneg", [8, S_TOK], F32)
        sp8 = cneg
        e8 = T1("f_e8", [8, 2, 512], F32)
        S.dma("pool", wfb[:], wview(w_in, 3584, 8))
        wload(S, wv, w_in, 3072, 512)
        S.dma("sp", negbf[:], W["fox_bf"])
        S.ts("dve", negbf[:], negbf[:], -1.0, None, ALU.mult)
        S.memset("pool", vb[:, :, 64:65], 1.0)
        for g in range(4):
            gc = slice(g * 512, (g + 1) * 512)
            pb = C.P[g % 2]
            for c in range(8):
                S.mm(pb[0:8, :], wfb[:, c, :], C.xT[:, c, gc], c == 0, c == 7)
            S.act(e8[:, g % 2, :], pb[0:8, :], AF.Exp, bias=negbf[:, 0:1], scale=-1.0)
            S.act(sp8[:, gc], e8[:, g % 2, :], AF.Ln, bias=1.0, scale=1.0)
        for g in range(4):
            gc = slice(g * 512, (g + 1) * 512)
            S.scan(cneg[:, gc], C.ones_f[0:8, :], sp8[:, gc], 0.0 if g == 0 else cneg[:, g * 512 - 1:g * 512])
        S.ts("dve", crow[:], cneg[:], -8.0, None, ALU.mult)
        pb = C.P[2]
        for t in range(NT):
            S.tr(pb[:, t * 8:(t + 1) * 8], cneg[0:8, t * 128:(t + 1) * 128], C.ident_f[0:8, 0:8])
        S.cp("act", cnegT[:].rearrange("p t h -> p (t h)"), pb[:, 0:128])
        for t in range(NT):
            pb = C.P[t % 2]
            for c in range(8):
                S.mm(pb[:, :], C.xT[:, c, t * 128:(t + 1) * 128], wv[:, c, :], c == 0, c == 7)
            S.cp("act", vb[:, t * 8:(t + 1) * 8, 0:64], pb[:, :].rearrange("p (h d) -> p h d", h=8))
        S.emit()
        st1.close()
        S = Sched(nc)
        st2 = st.enter_context(contextlib.ExitStack())
        T2 = mkT(nc, st2)
        wq = T2("f_wq", [128, 8, 512], BF16)
        wk = T2("f_wk", [128, 8, 512], BF16)
        qTp = T2("f_qTp", [128, 2, S_TOK], BF16)
        kTp = T2("f_kTp", [128, 2, S_TOK], BF16)
        PT = T2("f_PT", [128, 20, 512], BF16)
        rc = T2("f_rc", [128, 2, 4], F32)
        wload(S, wq, w_in, 2048, 512)
        wload(S, wk, w_in, 2560, 512)
        S.dma("pool", wo[:], w_out[512:1024, :].rearrange("(c p) n -> p c n", p=128))
        S.memset("pool", qTp[:], 0.0)
        S.memset("dve", kTp[:], 0.0)
        S.memset("dve", kTp[64:65, :, :], 1.0)
        ring = [0]
        kk = [0]

        def fx_proj(h):
            if h % 2 == 1:
                return
            for g in range(4):
                gc = slice(g * 512, (g + 1) * 512)
                pq = C.P[kk[0] % 2]
                kk[0] += 1
                for c in range(8):
                    S.mm(pq[:, :], wq[:, c, h * 64:(h + 2) * 64], C.xT[:, c, gc], c == 0, c == 7)
                S.cp("dve", qTp[0:64, 0, gc], pq[0:64, :])
                S.cp("dve", qTp[0:64, 1, gc], pq[64:128, :])
                pk = C.P[kk[0] % 2]
                kk[0] += 1
                for c in range(8):
                    S.mm(pk[:, :], wk[:, c, h * 64:(h + 2) * 64], C.xT[:, c, gc], c == 0, c == 7)
                S.cp("dve", kTp[0:64, 0, gc], pk[0:64, :])
                S.cp("dve", kTp[0:64, 1, gc], pk[64:128, :])
            S.dma("sp", qTp[64:65, 0, :], crow[h:h + 1, :])
            S.dma("sp", qTp[64:65, 1, :], crow[h + 1:h + 2, :])

        def fx_a(h, g):
            hs = h % 2
            i0 = 4 * g
            slots = {}
            for j in range(i0 + 4):
                base = max(j, i0)
                n = (i0 + 4 - base) * 128
                ps = C.P[2 + (kk[0] % 2)]
                kk[0] += 1
                S.mm(ps[:, 0:n], kTp[:, hs, j * 128:(j + 1) * 128], qTp[:, hs, base * 128:(i0 + 4) * 128], True, True)
                sl = ring[0] % 20
                ring[0] += 1
                S.act(PT[:, sl, 0:n], ps[:, 0:n], AF.Exp, bias=cnegT[:, j, h:h + 1], scale=SCALE)
                if j >= i0:
                    S.aselect(PT[:, sl, 0:128], PT[:, sl, 0:128], [[1, 128]], ALU.is_ge, 0.0, 0, -1)
                slots[j] = (sl, base)
            return slots

        def fx_b(h, g, slots):
            i0 = 4 * g
            po = C.P[4 + (g % 2)]
            for ii in range(4):
                i = i0 + ii
                for j in range(i + 1):
                    sl, base = slots[j]
                    S.mm(po[:, ii * 65:(ii + 1) * 65], PT[:, sl, (i - base) * 128:(i - base + 1) * 128],
                         vb[:, j * 8 + h, :], j == 0, j == i)
            pov = po[:, 0:260].rearrange("p (i d) -> p i d", d=65)
            S.recip(rc[:, g % 2, :], pov[:, :, 64])
            for ii in range(4):
                S.ts("dve", ybtm[:, i0 + ii, h * 64:(h + 1) * 64], po[:, ii * 65:ii * 65 + 64],
                     rc[:, g % 2, ii:ii + 1], None, ALU.mult)

        pend = None
        for h in range(8):
            for g in range(4):
                if g == 0:
                    fx_proj(h)
                need = 4 * g + 4
                if pend is not None and len(pend[2]) + need <= 20:
                    sl = fx_a(h, g)
                    fx_b(*pend)
                else:
                    if pend is not None:
                        fx_b(*pend)
                    sl = fx_a(h, g)
                pend = (h, g, sl)
        fx_b(*pend)
        S.emit()
        st2.close()
        S = Sched(nc)
        ytile = T("f_ytile", [128, 2, 4, 128], BF16)
        L = alloc_ln(nc, st, C, S, W["ln_g"], W["ln_b"], 0, 0) if dbg is None else None
        post_outproj_ln(S, C, L, ybtm, wo, ytile, dbg)
        S.emit()


def phase_ln(nc, C, W, layer, which, y=None):
    with contextlib.ExitStack() as st:
        S = Sched(nc)
        L = alloc_ln(nc, st, C, S, W["ln_g"], W["ln_b"], layer, which)
        for t in range(NT):
            ln_tile(S, C, L, t, to_xT=(y is None))
            if y is not None:
                S.dma("sp", y[t * 128:(t + 1) * 128, :], C.res[:, t, :])
        S.emit()


def phase_ffn(nc, C, W, layer, y=None):
    w13, w2 = W["ffn_w13"], W["ffn_w2"]
    groups = [(0, 6), (6, 12), (12, 18), (18, 22)]
    with contextlib.ExitStack() as st:
        T = mkT(nc, st)
        S = Sched(nc)
        L = alloc_ln(nc, st, C, S, W["ln_g"], W["ln_b"], layer, 1)
        w2b = T("n_w2b", [128, 2, 6, 1024], BF16)
        hT = T("n_hT", [128, 6, S_TOK], BF16)
        wt = T("n_wt", [128, 3, 8, 2, 256], BF16)
        sa = T("n_sa", [128, 2, 512], F32)
        w2v = w2[layer].rearrange("(f p) n -> p f n", p=128)
        def w13_block(wb, f, ff, tb, f0):
            tcs = slice(tb * 512, (tb + 1) * 512)
            pa = C.P[kk[0] % 2]
            pb = C.P[2 + kk[0] % 2]
            sb_ = kk[0] % 2
            kk[0] += 1
            for c in range(8):
                S.mm(pa[:, :], wt[:, wb, c, 0, ff * 128:(ff + 1) * 128], C.xT[:, c, tcs], c == 0, c == 7)
            for c in range(8):
                S.mm(pb[:, :], wt[:, wb, c, 1, ff * 128:(ff + 1) * 128], C.xT[:, c, tcs], c == 0, c == 7)
            S.act(sa[:, sb_, :], pa[:, :], AF.Silu)
            S.tt("dve", hT[:, f - f0, tcs], sa[:, sb_, :], pb[:, :], ALU.mult)

        def w2_tile(gi, ng, t, last):
            for half in range(2):
                py = C.P[4 + ky[0] % 2]
                ky[0] += 1
                for fl in range(ng):
                    S.mm(py[:, :], hT[:, fl, t * 128:(t + 1) * 128], w2b[:, gi % 2, fl, half * 512:(half + 1) * 512],
                         fl == 0, fl == ng - 1)
                r = C.res[:, t, half * 512:(half + 1) * 512]
                if gi == 0:
                    S.stt(r, r, ALPHA, py[:, :], ALU.mult, ALU.add)
                else:
                    S.tt("dve", r, r, py[:, :], ALU.add)
            if last:
                ln_stats(S, C, L, t)
                if t >= 1:
                    ln_apply(S, C, L, t - 1, to_xT=(y is None), y=y, eng="mix")
                if t == NT - 1:
                    ln_apply(S, C, L, t, to_xT=(y is None), y=y, eng="mix")

        kk = [0]
        ky = [0]
        k = 0
        for gi, (f0, f1) in enumerate(groups):
            ng = f1 - f0
            last = gi == len(groups) - 1
            wbs = {}
            for fp in range(f0 // 2, f1 // 2):
                wb = k % 3
                k += 1
                wbs[fp] = wb
                if k == 1:
                    va = wview(w13[layer], fp * 256, 256)
                    vb_ = wview(w13[layer], DFF + fp * 256, 256)
                    for cc in range(0, 8, 2):
                        S.dma("pool", wt[:, wb, cc:cc + 2, 0, :], va[:, cc:cc + 2, :])
                    for cc in range(0, 8, 2):
                        S.dma("pool", wt[:, wb, cc:cc + 2, 1, :], vb_[:, cc:cc + 2, :])
                else:
                    S.dma("pool", wt[:, wb, :, 0, :], wview(w13[layer], fp * 256, 256))
                    S.dma("pool", wt[:, wb, :, 1, :], wview(w13[layer], DFF + fp * 256, 256))
                fl0 = 2 * fp - f0
                S.dma("pool", w2b[:, gi % 2, fl0:fl0 + 2, :], w2v[:, 2 * fp:2 * fp + 2, :])
                for ff in range(2):
                    for tb in range(4):
                        w13_block(wb, 2 * fp + ff, ff, tb, f0)
            for t in range(NT):
                w2_tile(gi, ng, t, last)
        S.emit()


def phase_rglru(nc, C, W, dbg=None):
    w_in, w_out = W["od_w_in"], W["od_w_out"]
    GK = 2.0 * math.sqrt(2.0 / math.pi)
    with contextlib.ExitStack() as st:
        T = mkT(nc, st)
        S = Sched(nc)
        wx = T("r_wx", [128, 8, 512], BF16)
        wg = T("r_wg", [128, 8, 512], BF16)
        wo = T("r_wo", [128, 4, 1024], BF16)
        bda = T("r_bda", [128, 4, 128], BF16)
        bdx = T("r_bdx", [128, 4, 128], BF16)
        cw = T("r_cw", [128, 4, 4], F32)
        cb = T("r_cb", [128, 4], F32)
        ba = T("r_ba", [128, 4], F32)
        bx = T("r_bx", [128, 4], F32)
        lam = T("r_lam", [128, 4], F32)
        sp = T("r_sp", [128, 6, 4], F32)
        xr = T("r_xr", [128, 3 + S_TOK], F32)
        gl = T("r_gl", [128, 2, S_TOK], BF16)
        om = T("r_om", [128, S_TOK], F32)
        xc = T("r_xc", [128, S_TOK], F32)
        xcb = T("r_xcb", [128, S_TOK], BF16)
        at = T("r_at", [128, S_TOK], F32)
        ut = T("r_ut", [128, S_TOK], F32)
        tg = T("r_tg", [128, 2, 512], F32)
        tq = T("r_tq", [128, 2, 512], F32)
        yc = T("r_yc", [128, 4, S_TOK], BF16)
        wload(S, wx, w_in, 0, 512)
        wload(S, wg, w_in, 512, 512)
        S.dma("pool", wo[:], w_out[0:512, :].rearrange("(c p) n -> p c n", p=128))
        S.dma("pool", bda[:], W["bd_a"])
        S.dma("pool", bdx[:], W["bd_x"])
        S.dma("sp", cw[:], W["cw"])
        S.dma("sp", cb[:], W["cb"])
        S.dma("sp", ba[:], W["ba"])
        S.dma("sp", bx[:], W["bx"])
        S.dma("sp", lam[:], W["lam"])
        y_, s_, s2, pl, t0 = sp[:, 0, :], sp[:, 1, :], sp[:, 2, :], sp[:, 3, :], sp[:, 4, :]
        S.act(y_, lam[:], AF.Exp, scale=-1.0)
        S.ts("dve", t0, y_, 2.0, None, ALU.add)
        S.recip(t0, t0)
        S.tt("dve", s_, y_, t0, ALU.mult)
        S.tt("dve", s2, s_, s_, ALU.mult)
        S.ts("dve", pl, s2, 1.0 / 13.0, 1.0 / 11.0, ALU.mult, ALU.add)
        for cf in (1.0 / 9.0, 1.0 / 7.0, 1.0 / 5.0, 1.0 / 3.0, 1.0):
            S.tt("dve", pl, pl, s2, ALU.mult)
            S.ts("dve", pl, pl, cf, None, ALU.add)
        S.tt("dve", pl, pl, s_, ALU.mult)
        m8 = sp[:, 5, :]
        S.ts("dve", m8, pl, -16.0, None, ALU.mult)
        S.ts("dve", t0, pl, -32.0, None, ALU.mult)
        m16 = t0
        S.memset("pool", xr[:, 0:3], 0.0)
        kc = [0]

        def st_g(ch):
            for g in range(4):
                gc = slice(g * 512, (g + 1) * 512)
                k = kc[0]
                kc[0] += 1
                pa = C.P[k % 2]
                pb = C.P[2 + k % 2]
                b = k % 2
                for c in range(8):
                    S.mm(pa[:, :], wx[:, c, ch * 128:(ch + 1) * 128], C.xT[:, c, gc], c == 0, c == 7)
                for c in range(8):
                    S.mm(pb[:, :], wg[:, c, ch * 128:(ch + 1) * 128], C.xT[:, c, gc], c == 0, c == 7)
                S.cp("act", xr[:, 3 + g * 512:3 + (g + 1) * 512], pa[:, :])
                S.cp("act", tg[:, b, :], pb[:, :])
                S.act(tq[:, b, :], pb[:, :], AF.Square)
                S.ts("dve", tq[:, b, :], tq[:, b, :], 0.044715, 1.0, ALU.mult, ALU.add)
                S.tt("dve", tq[:, b, :], tq[:, b, :], tg[:, b, :], ALU.mult)
                S.act(tq[:, b, :], tq[:, b, :], AF.Sigmoid, scale=GK)
                S.tt("dve", gl[:, ch % 2, gc], tq[:, b, :], tg[:, b, :], ALU.mult)

        def st_conv(ch):
            S.ts("dve", xc[:], xr[:, 0:S_TOK], cw[:, ch, 0:1], cb[:, ch:ch + 1], ALU.mult, ALU.add)
            for j in range(1, 4):
                S.stt(xc[:], xr[:, j:j + S_TOK], cw[:, ch, j:j + 1], xc[:], ALU.mult, ALU.add)
            S.cp("act", xcb[:], xc[:])

        def st_gates(ch):
            for g in range(4):
                gc = slice(g * 512, (g + 1) * 512)
                pa = C.P[4]
                pb = C.P[5]
                S.mm(pa[:, :], bda[:, ch, :], xcb[:, gc], True, True)
                S.mm(pb[:, :], bdx[:, ch, :], xcb[:, gc], True, True)
                S.act(at[:, gc], pa[:, :], AF.Sigmoid, bias=ba[:, ch:ch + 1], scale=1.0)
                S.act(ut[:, gc], pb[:, :], AF.Sigmoid, bias=bx[:, ch:ch + 1], scale=1.0)
                S.tt("pool", ut[:, gc], ut[:, gc], xc[:, gc], ALU.mult)
            for g in range(4):
                gc = slice(g * 512, (g + 1) * 512)
                S.act(om[:, gc], at[:, gc], AF.Exp, scale=m16[:, ch:ch + 1])
                S.act(at[:, gc], at[:, gc], AF.Exp, scale=m8[:, ch:ch + 1])
                S.ts("dve", om[:, gc], om[:, gc], -1.0, 1.0, ALU.mult, ALU.add)
            for g in range(4):
                gc = slice(g * 512, (g + 1) * 512)
                S.act(om[:, gc], om[:, gc], AF.Sqrt)
                S.tt("dve", ut[:, gc], om[:, gc], ut[:, gc], ALU.mult)

        def st_fin(ch):
            S.scan(xc[:], at[:], ut[:], 0.0)
            S.tt("pool", yc[:, ch, :], xc[:], gl[:, ch % 2, :], ALU.mult)

        st_g(0)
        st_conv(0)
        for ch in range(4):
            if ch < 3:
                st_g(ch + 1)
            st_gates(ch)
            st_fin(ch)
            if ch < 3:
                st_conv(ch + 1)
        if dbg is not None:
            for ch in range(4):
                S.dma("sp", dbg[ch * 128:(ch + 1) * 128, :], yc[:, ch, :])
        outproj_partial(S, C, lambda t, c: yc[:, c, t * 128:(t + 1) * 128], 4, wo, first=True)
        S.emit()


def phase_sb(nc, C, W, dbg=None):
    w_in, w_out = W["od_w_in"], W["od_w_out"]
    SCALE = 0.125
    with contextlib.ExitStack() as st:
        T = mkT(nc, st)
        qT = T("s_qT", [128, 4, S_TOK], BF16)
        kT = T("s_kT", [128, 4, S_TOK], BF16)
        vd = T("s_vd", [128, NT, 512], BF16)
        yd = T("s_yd", [128, NT, 512], BF16)
        with contextlib.ExitStack() as st1:
            T1 = mkT(nc, st1)
            S = Sched(nc)
            wq = T1("s_wq", [128, 8, 512], BF16)
            wk = T1("s_wk", [128, 8, 512], BF16)
            wv = T1("s_wv", [128, 8, 512], BF16)
            wload(S, wq, w_in, 1024, 512)
            wload(S, wk, w_in, 1536, 512)
            S.dma("pool", wv[:], wview(w_in, 2048, 512))
            k = 0
            for c4 in range(4):
                for g in range(4):
                    gc = slice(g * 512, (g + 1) * 512)
                    pa = C.P[k % 2]
                    pb = C.P[2 + k % 2]
                    k += 1
                    for c in range(8):
                        S.mm(pa[:, :], wq[:, c, c4 * 128:(c4 + 1) * 128], C.xT[:, c, gc], c == 0, c == 7)
                    for c in range(8):
                        S.mm(pb[:, :], wk[:, c, c4 * 128:(c4 + 1) * 128], C.xT[:, c, gc], c == 0, c == 7)
                    S.cp("act", qT[:, c4, gc], pa[:, :])
                    S.cp("dve", kT[:, c4, gc], pb[:, :])
            for t in range(NT):
                pb = C.P[4 + t % 2]
                for c in range(8):
                    S.mm(pb[:, :], C.xT[:, c, t * 128:(t + 1) * 128], wv[:, c, :], c == 0, c == 7)
                S.cp("act", vd[:, t, :], pb[:, :])
            S.emit()
        with contextlib.ExitStack() as st2:
            T2 = mkT(nc, st2)
            S = Sched(nc)
            A = T2("s_A", [128, 3072], F32)
            CF = T2("s_CF", [128, 3088], F32)
            wt = T2("s_wt", [128, 2, S_TOK], BF16)
            wTt = T2("s_wTt", [128, 2, S_TOK], BF16)
            ntot = T2("s_ntot", [128, 4, 1], F32)
            posA = [0]
            posC = [0]
            zb = [0]
            qn = [0]
            order_a = []
            order_b = []
            for k in range(8):
                order_a += [15 - k, k]
                order_b += [8 + k, 7 - k]
            work = [(h, i) for h in range(8) for i in (order_a if h % 2 == 0 else order_b)]
            prevA = [None]
            prevC = [None]

            def stage_a(it, h, i):
                c4 = h // 2
                pr = slice(64 * (h % 2), 64 * (h % 2) + 64)
                n = (i + 1) * 128
                nkb = (n + 511) // 512
                if it % 2 == 0:
                    a0, c0 = 0, 0
                else:
                    assert n <= 1024
                    a0, c0 = 2048, 2056
                S.memset("pool", CF[:, c0:c0 + 1], 0.0)
                for kb in range(nkb):
                    w = min(512, n - kb * 512)
                    k0 = kb * 512
                    pz = C.P[zb[0] % 5]
                    zb[0] += 1
                    S.mm(pz[:, 0:w], qT[pr, c4, i * 128:(i + 1) * 128], kT[pr, c4, k0:k0 + w], True, True)
                    Ac = A[:, a0 + k0:a0 + k0 + w]
                    S.act(Ac, pz[:, 0:w], AF.Exp, scale=SCALE)
                    S.act(Ac, Ac, AF.Ln, bias=1.0, scale=1.0)
                    if kb == nkb - 1:
                        dgA = A[:, a0 + i * 128:a0 + (i + 1) * 128]
                        S.aselect(dgA, dgA, [[-1, 128]], ALU.is_gt, 0.0, 0, 1)
                    S.scan(CF[:, c0 + 1 + k0:c0 + 1 + k0 + w], C.ones_f[:, 0:w], Ac, CF[:, c0 + k0:c0 + k0 + 1])
                    S.stt(Ac, pz[:, 0:w], SCALE, CF[:, c0 + k0:c0 + k0 + w], ALU.mult, ALU.add)
                nt_ = ntot[:, it % 4, :]
                S.ts("dve", nt_, CF[:, c0 + n:c0 + n + 1], -1.0, None, ALU.mult)
                return (a0, nt_)

            def stage_b(it, h, i, a0, nt_):
                n = (i + 1) * 128
                wb = it % 2
                S.act(wt[:, wb, 0:n], A[:, a0:a0 + n], AF.Exp, bias=nt_, scale=1.0)
                dg = slice(i * 128, (i + 1) * 128)
                S.aselect(wt[:, wb, dg], wt[:, wb, dg], [[-1, 128]], ALU.is_gt, 0.0, 0, 1)
                src = wt[:, wb, 0:n]
                dst = wTt[:, wb, 0:n].rearrange("p (j t) -> p j t", t=128)
                S._add("sp", lambda e: e.dma_start_transpose(out=dst, in_=src), [src], [wTt[:, wb, 0:n]],
                       is_dma=True, tag=("wTt", wb))

            def stage_c(it, h, i):
                wb = it % 2
                po = C.P[5]
                for j in range(i + 1):
                    S.mm(po[:, 0:64], wTt[:, wb, j * 128:(j + 1) * 128], vd[:, j, h * 64:(h + 1) * 64], j == 0, j == i)
                S.cp("act", yd[:, i, h * 64:(h + 1) * 64], po[:, 0:64])

            pend_b = None
            pend_c = None
            for it, (h, i) in enumerate(work):
                ra = stage_a(it, h, i)
                if pend_b is not None:
                    stage_b(*pend_b)
                if pend_c is not None:
                    stage_c(*pend_c)
                pend_c = pend_b[:3] if pend_b is not None else None
                pend_b = (it, h, i) + ra
            stage_b(*pend_b)
            if pend_c is not None:
                stage_c(*pend_c)
            stage_c(*pend_b[:3])
            S.emit()
        with contextlib.ExitStack() as st3:
            T3 = mkT(nc, st3)
            S = Sched(nc)
            wo = T3("s_wo", [128, 4, 1024], BF16)
            ytile = T3("s_ytile", [128, 2, 4, 128], BF16)
            L = alloc_ln(nc, st3, C, S, W["ln_g"], W["ln_b"], 1, 0) if dbg is None else None
            wov = w_out[512:1024, :].rearrange("(c p) n -> p c n", p=128)
            for c in range(4):
                S.dma("pool", wo[:, c, :], wov[:, c, :])
            post_outproj_ln(S, C, L, yd, wo, ytile, dbg)
            S.emit()


DEBUG_MAP = {"hgrn": [("l0_ya", "dbg")], "fox": [("l0_yb", "dbg")], "l0mix": [("l0_x1", "y")],
             "l0": [("l0_x2", "y")], "full": [("l1_x4", "y")],
             "rglru": [("l1_ycT", "dbg")], "sb": [("l1_yd", "dbg")]}


def build(stage="full"):
    nc = bass.Bass("TRN2", target_bir_lowering=False)
    dt_in = lambda name, shape: nc.dram_tensor(name, list(shape), F32, kind="ExternalInput").ap()
    x = dt_in("x", [S_TOK, D])
    W = {}
    W["ev_w_in"] = dt_in("ev_w_in", [D, 3592])
    W["lb3"] = dt_in("lb3", [128, 3, 4])
    W["normg"] = dt_in("normg", [1, 256])
    W["fox_bf"] = dt_in("fox_bf", [8, 1])
    W["ev_w_out"] = dt_in("ev_w_out", [D, D])
    W["ln_g"] = dt_in("ln_g", [2, 2, D])
    W["ln_b"] = dt_in("ln_b", [2, 2, D])
    W["ffn_w13"] = dt_in("ffn_w13", [2, D, 2 * DFF])
    W["ffn_w2"] = dt_in("ffn_w2", [2, DFF, D])
    W["od_w_in"] = dt_in("od_w_in", [D, 2560])
    W["od_w_out"] = dt_in("od_w_out", [D, D])
    W["bd_a"] = dt_in("bd_a", [128, 4, 128])
    W["bd_x"] = dt_in("bd_x", [128, 4, 128])
    W["cw"] = dt_in("cw", [128, 4, 4])
    W["cb"] = dt_in("cb", [128, 4])
    W["ba"] = dt_in("ba", [128, 4])
    W["bx"] = dt_in("bx", [128, 4])
    W["lam"] = dt_in("lam", [128, 4])
    dbg = None
    if stage in ("hgrn", "fox", "sb"):
        dbg = nc.dram_tensor("dbg", [S_TOK, 512], BF16, kind="ExternalOutput").ap()
    if stage == "rglru":
        dbg = nc.dram_tensor("dbg", [512, S_TOK], BF16, kind="ExternalOutput").ap()
    y = nc.dram_tensor("y", [S_TOK, D], F32, kind="ExternalOutput").ap()

    with contextlib.ExitStack() as st:
        T = mkT(nc, st)
        C = Ctx()
        C.res = T("res", [128, NT, D], F32)
        C.xT = T("xT", [128, 8, S_TOK], BF16)
        C.ident_f = T("ident_f", [128, 128], F32)
        C.ident_b = T("ident_b", [128, 128], BF16)
        C.ones_f = T("ones_f", [128, 512], F32)
        C.P = [st.enter_context(nc.psum_tensor("P%d" % i, [128, 512], F32)) for i in range(6)]
        C.Q = [st.enter_context(nc.psum_tensor("Q%d" % i, [128, 1024], BF16)) for i in range(2)]
        C.P = C.P + [C.P[4], C.P[5]]

        def dump_res():
            S = Sched(nc)
            yv = y.rearrange("(t p) d -> p t d", p=128)
            for i in range(4):
                S.dma("sp", yv[:, 4 * i:4 * i + 4, :], C.res[:, 4 * i:4 * i + 4, :])
            S.emit()

        merged = stage in ("full", "l0", "l0mix", "hgrn")
        if not merged:
            phase_load(nc, C, x)
        if stage.startswith("p:"):
            plist = stage[2:].split(",")
            wrote = False
            for p in plist:
                if p == "hgrn0":
                    phase_hgrn_pair(nc, C, W, 0)
                elif p == "hgrn1":
                    phase_hgrn_pair(nc, C, W, 1)
                elif p == "fox":
                    phase_fox(nc, C, W)
                elif p == "ln00":
                    pass
                elif p == "ffn0":
                    phase_ffn(nc, C, W, 0)
                elif p == "rglru":
                    phase_rglru(nc, C, W)
                elif p == "sb":
                    phase_sb(nc, C, W)
                elif p == "ln10":
                    pass
                elif p == "ffn1":
                    phase_ffn(nc, C, W, 1, y)
                    wrote = True
            if not wrote:
                dump_res()
            return nc
        if stage in ("rglru", "sb"):
            if stage == "rglru":
                phase_rglru(nc, C, W, dbg)
            else:
                phase_sb(nc, C, W, dbg)
            dump_res()
            return nc
        if stage != "fox":
            phase_hgrn_pair(nc, C, W, 0, dbg if stage == "hgrn" else None, x=x)
            phase_hgrn_pair(nc, C, W, 1, dbg if stage == "hgrn" else None)
        if stage == "hgrn":
            dump_res()
            return nc
        phase_fox(nc, C, W, dbg if stage == "fox" else None)
        if stage == "fox":
            dump_res()
            return nc
        if stage == "l0mix":
            dump_res()
            return nc
        phase_ffn(nc, C, W, 0, y if stage == "l0" else None)
        if stage == "l0":
            return nc
        phase_rglru(nc, C, W)
        phase_sb(nc, C, W)
        phase_ffn(nc, C, W, 1, y)
    return nc


def _blockdiag(w):
    w = np.asarray(w, dtype=np.float32)
    out = np.zeros((128, 4, 128), dtype=np.float32)
    for n in range(8):
        ch, o = n // 2, 64 * (n % 2)
        out[o:o + 64, ch, o:o + 64] = w[n]
    return out


def host_inputs(inputs):
    f = lambda a: np.ascontiguousarray(np.asarray(a, dtype=np.float32))
    shared = {
        "ev_w_in": f(inputs["ev_w_in"][0]),
        "lb3": f(np.asarray(inputs["hgrn_lb"]).reshape(3, 4, 128).transpose(2, 0, 1)),
        "normg": f(np.tile(np.asarray(inputs["ev_hgrn_norm_g"][0]), 2)[None, :]),
        "ev_w_out": f(inputs["ev_w_out"][0]),
        "fox_bf": f(np.asarray(inputs["ev_fox_bf"][0]).reshape(8, 1)),
        "ffn_w13": f(inputs["ffn_w13"]),
        "ffn_w2": f(inputs["ffn_w2"]),
        "od_w_in": f(inputs["od_w_in"][0]),
        "od_w_out": f(inputs["od_w_out"][0]),
        "bd_a": f(_blockdiag(inputs["od_gate_a_w"][0])),
        "bd_x": f(_blockdiag(inputs["od_gate_x_w"][0])),
        "cw": f(np.asarray(inputs["od_conv_w"][0]).reshape(4, 4, 128).transpose(2, 1, 0)),
        "cb": f(np.asarray(inputs["od_conv_b"][0]).reshape(4, 128).T),
        "ba": f(np.asarray(inputs["od_gate_a_b"][0]).reshape(4, 128).T),
        "bx": f(np.asarray(inputs["od_gate_x_b"][0]).reshape(4, 128).T),
        "lam": f(np.asarray(inputs["od_lru_lambda"][0]).reshape(4, 128).T),
        "ln_g": f(inputs["ln_g"]),
        "ln_b": f(inputs["ln_b"]),
    }
    return shared


def kernel(**inputs):
    shared = host_inputs(inputs)
    x = np.asarray(inputs["x"], dtype=np.float32)
    nc = build("full")
    in_maps = [dict(shared, x=np.ascontiguousarray(x[i])) for i in range(8)]
    res = run_bass_kernel_spmd(nc, in_maps, core_ids=list(range(8)))
    return np.stack([r["y"] for r in res.results], axis=0)
```
